# Optimizing a Trainium2 kernel written in Bass

```python
import jax, jax.numpy as jnp
from jax import lax
import numpy as np

D_MODEL = 1024
BATCH = 8
SEQ = 4096
DEPTH = 1

CHUNK = 64
LEFT_CHUNKS = 8
BAND = (LEFT_CHUNKS + 1) * CHUNK
N_HEADS = 8
HEAD_DIM = 64
D_ATTN = N_HEADS * HEAD_DIM
D_CONV = 512
CONV_K = 31
MAX_REL = 128
D_FF = 2816
FFN_CONV_K = 3
EPS = 1e-6
NEG_INF = -1e30
IN_WIDTHS = (D_ATTN, D_ATTN, D_ATTN, D_CONV, D_CONV, D_MODEL, D_MODEL)
D_IN = D_ATTN * 3 + D_CONV * 2 + D_MODEL * 2

kernel_name = "hybrid_chunked_attn_conformer_conv_convffn_block"


def rms_norm(x, g):
    xf = x.astype(jnp.float32)
    y = xf * lax.rsqrt(jnp.mean(xf * xf, axis=-1, keepdims=True) + EPS)
    return (y * g.astype(jnp.float32)).astype(x.dtype)


def layer_norm(x, g, b):
    xf = x.astype(jnp.float32)
    mu = jnp.mean(xf, axis=-1, keepdims=True)
    var = jnp.mean(jnp.square(xf - mu), axis=-1, keepdims=True)
    y = (xf - mu) * lax.rsqrt(var + EPS)
    return (y * g.astype(jnp.float32) + b.astype(jnp.float32)).astype(x.dtype)


def causal_dwconv(x, w, b):
    k = w.shape[0]
    y = lax.conv_general_dilated(
        x, w[:, None, :].astype(x.dtype), window_strides=(1,), padding=[(k - 1, 0)],
        dimension_numbers=('NWC', 'WIO', 'NWC'), feature_group_count=x.shape[-1])
    return y + b


def chunk_band(t):
    b, s, h, dh = t.shape
    nc = s // CHUNK
    tc = t.reshape(b, nc, CHUNK, h, dh)
    tp = jnp.pad(tc, ((0, 0), (LEFT_CHUNKS, 0), (0, 0), (0, 0), (0, 0)))
    band = jnp.stack([tp[:, j:j + nc] for j in range(LEFT_CHUNKS + 1)], axis=2)
    return band.reshape(b, nc, BAND, h, dh)


def chunked_rel_attention(q, k, v, rel_bias):
    b, s, _ = q.shape
    nc = s // CHUNK
    qc = q.reshape(b, nc, CHUNK, N_HEADS, HEAD_DIM)
    kb = chunk_band(k.reshape(b, s, N_HEADS, HEAD_DIM))
    vb = chunk_band(v.reshape(b, s, N_HEADS, HEAD_DIM))
    scores = jnp.einsum('bcqhd,bckhd->bhcqk', qc, kb).astype(jnp.float32) * (HEAD_DIM ** -0.5)
    qi = jnp.arange(CHUNK)
    kj = jnp.arange(BAND)
    rel = LEFT_CHUNKS * CHUNK + qi[:, None] - kj[None, :]
    idx = jnp.clip(rel, -MAX_REL, MAX_REL) + MAX_REL
    bias = rel_bias[:, idx].astype(jnp.float32)
    scores = scores + bias[None, :, None, :, :]
    key_chunk = jnp.arange(nc)[:, None] - LEFT_CHUNKS + (kj // CHUNK)[None, :]
    valid = key_chunk >= 0
    scores = jnp.where(valid[None, None, :, None, :], scores, NEG_INF)
    probs = jax.nn.softmax(scores, axis=-1).astype(v.dtype)
    out = jnp.einsum('bhcqk,bckhd->bcqhd', probs, vb)
    return out.reshape(b, s, D_ATTN)


def token_mixer(h, w_in, b_in, rel_bias, w_attn_o, w_dw, b_dw, g_ln, b_ln,
                w_conv_o, b_conv_o, w_mix_o):
    z = h @ w_in + b_in
    splits = list(np.cumsum(IN_WIDTHS)[:-1])
    q, k, v, glu_a, glu_b, gate_a, gate_b = jnp.split(z, splits, axis=-1)
    a = chunked_rel_attention(q, k, v, rel_bias) @ w_attn_o
    u = glu_a * jax.nn.sigmoid(glu_b)
    u = causal_dwconv(u, w_dw, b_dw)
    u = jax.nn.silu(layer_norm(u, g_ln, b_ln))
    cb = u @ w_conv_o + b_conv_o
    y = jax.nn.sigmoid(gate_a) * a + jax.nn.sigmoid(gate_b) * cb
    return y @ w_mix_o


def conv_ffn(h, w_up, w_dw, b_dw, w_down):
    u = causal_dwconv(h @ w_up, w_dw, b_dw)
    val, gt = jnp.split(u, 2, axis=-1)
    return (jax.nn.gelu(gt) * val) @ w_down


def setup_inputs(seed: int = 0) -> dict:
    key = jax.random.key(seed)
    ks = jax.random.split(key, 26)
    L = DEPTH

    def nrm(k, shape, scale):
        return jax.random.normal(k, shape, jnp.float32) * scale

    def gain(k, shape):
        return 1.0 + 0.1 * jax.random.normal(k, shape, jnp.float32)

    return {
        "x": nrm(ks[0], (BATCH, SEQ, D_MODEL), 1.0),
        "c": nrm(ks[1], (BATCH, D_MODEL), 1.0),
        "w_ada": nrm(ks[2], (L, D_MODEL, 6 * D_MODEL), 0.5 * D_MODEL ** -0.5),
        "b_ada": nrm(ks[3], (L, 6 * D_MODEL), 0.01),
        "g_pre_mix": gain(ks[4], (L, D_MODEL)),
        "g_post_mix": gain(ks[5], (L, D_MODEL)),
        "w_in": nrm(ks[6], (L, D_MODEL, D_IN), D_MODEL ** -0.5),
        "b_in": nrm(ks[7], (L, D_IN), 0.01),
        "rel_bias": nrm(ks[8], (L, N_HEADS, 2 * MAX_REL + 1), 0.5),
        "w_attn_o": nrm(ks[9], (L, D_ATTN, D_MODEL), D_ATTN ** -0.5),
        "w_dw_conv": nrm(ks[10], (L, CONV_K, D_CONV), CONV_K ** -0.5),
        "b_dw_conv": nrm(ks[11], (L, D_CONV), 0.01),
        "g_conv_ln": gain(ks[12], (L, D_CONV)),
        "b_conv_ln": nrm(ks[13], (L, D_CONV), 0.01),
        "w_conv_o": nrm(ks[14], (L, D_CONV, D_MODEL), D_CONV ** -0.5),
        "b_conv_o": nrm(ks[15], (L, D_MODEL), 0.01),
        "w_mix_o": nrm(ks[16], (L, D_MODEL, D_MODEL), D_MODEL ** -0.5),
        "g_pre_ffn": gain(ks[17], (L, D_MODEL)),
        "g_post_ffn": gain(ks[18], (L, D_MODEL)),
        "w_up": nrm(ks[19], (L, D_MODEL, 2 * D_FF), D_MODEL ** -0.5),
        "w_dw_ffn": nrm(ks[20], (L, FFN_CONV_K, 2 * D_FF), FFN_CONV_K ** -0.5),
        "b_dw_ffn": nrm(ks[21], (L, 2 * D_FF), 0.01),
        "w_down": nrm(ks[22], (L, D_FF, D_MODEL), D_FF ** -0.5),
    }


def reference(x, c, w_ada, b_ada, g_pre_mix, g_post_mix, w_in, b_in, rel_bias,
              w_attn_o, w_dw_conv, b_dw_conv, g_conv_ln, b_conv_ln, w_conv_o,
              b_conv_o, w_mix_o, g_pre_ffn, g_post_ffn, w_up, w_dw_ffn, b_dw_ffn,
              w_down):
    c_act = jax.nn.silu(c)
    for l in range(DEPTH):
        mod = c_act @ w_ada[l] + b_ada[l]
        sh_m, sc_m, gt_m, sh_f, sc_f, gt_f = [m[:, None, :] for m in jnp.split(mod, 6, axis=-1)]
        h = rms_norm(x, g_pre_mix[l]) * (1.0 + sc_m) + sh_m
        y = token_mixer(h, w_in[l], b_in[l], rel_bias[l], w_attn_o[l], w_dw_conv[l],
                        b_dw_conv[l], g_conv_ln[l], b_conv_ln[l], w_conv_o[l],
                        b_conv_o[l], w_mix_o[l])
        x = x + gt_m * rms_norm(y, g_post_mix[l])
        h = rms_norm(x, g_pre_ffn[l]) * (1.0 + sc_f) + sh_f
        y = conv_ffn(h, w_up[l], w_dw_ffn[l], b_dw_ffn[l], w_down[l])
        x = x + gt_f * rms_norm(y, g_post_ffn[l])
    return x
```

```python
import numpy as np
from contextlib import ExitStack
import concourse.bass as bass
import concourse.mybir as mybir
from concourse.bass_utils import run_bass_kernel_spmd

F32 = mybir.dt.float32
BF16 = mybir.dt.bfloat16
AF = mybir.ActivationFunctionType
ALU = mybir.AluOpType

D = 1024
SEQ = 4096
NCORES = 8
DIN = 4608
DFF = 2816
EPS = 1e-6
T = 256
NT = SEQ // T
NEG = -30000.0

O_BADA, O_GPM, O_GPF, O_BIN, O_WDW, O_BDW, O_GLN, O_BLN, O_BCO, O_WFF, O_BFF = (
    0, 48, 56, 64, 100, 224, 228, 232, 236, 244, 376)
NCOLV = 420
R_GPOSTM, R_GPOSTF, R_BGTM, R_BGTF, R_BV = 0, 1024, 2048, 3072, 4096
NROWV = 4608


class Buf:
    __slots__ = ("name", "w", "r")

    def __init__(self, name=""):
        self.name = name
        self.w = None
        self.r = {}


class Sched:
    ENG = ("pe", "act", "dve", "pool", "sp")

    def __init__(self, nc, es, n_dma_sems=28):
        self.nc = nc
        self.prog = {e: [] for e in self.ENG}
        self.sem = {e: es.enter_context(nc.semaphore("s_" + e)) for e in self.ENG}
        self.cnt = {e: 0 for e in self.ENG}
        self.flag = {e: set() for e in self.ENG}
        self.dsem = [es.enter_context(nc.semaphore("d%d" % i)) for i in range(n_dma_sems)]
        self.dcnt = [0] * n_dma_sems
        self.dnx = {}
        self.known = {e: {} for e in self.ENG}

    def _need(self, e, key, val, waits):
        if key == e and e == "pe":
            return
        if self.known[e].get(key, 0) >= val:
            return
        if waits.get(key, 0) < val:
            waits[key] = val

    def _deps(self, e, reads, writes):
        waits = {}
        for b in reads:
            if b.w is not None:
                self._need(e, b.w[0], b.w[1], waits)
        for b in writes:
            if b.w is not None:
                self._need(e, b.w[0], b.w[1], waits)
            for k, v in b.r.items():
                self._need(e, k, v, waits)
        return waits

    def _emit_waits(self, e, waits):
        for key, val in waits.items():
            self.prog[e].append(("w", key, val))
            self.known[e][key] = val
            if isinstance(key, str):
                self.flag[key].add(val)

    def op(self, e, fn, reads=(), writes=()):
        self._emit_waits(e, self._deps(e, reads, writes))
        self.cnt[e] += 1
        v = self.cnt[e]
        self.prog[e].append(("o", fn, v))
        for b in reads:
            b.r[e] = v
        for b in writes:
            b.w = (e, v)
            b.r = {}
        return v

    def dma(self, e, fn, reads=(), writes=()):
        lo, hi = (0, 16) if e == "sp" else (16, len(self.dsem))
        i = self.dnx.get(e, lo)
        self.dnx[e] = lo + (i + 1 - lo) % (hi - lo)
        waits = self._deps(e, reads, writes)
        if self.dcnt[i] > 0:
            self._need(e, i, self.dcnt[i], waits)
        self._emit_waits(e, waits)
        self.dcnt[i] += 16
        v = self.dcnt[i]
        self.prog[e].append(("d", fn, i))
        for b in reads:
            b.r[i] = v
        for b in writes:
            b.w = (i, v)
            b.r = {}

    def barrier(self):
        for e in self.ENG:
            waits = {}
            for k in self.ENG:
                if k != e and self.cnt[k] > 0:
                    self._need(e, k, self.cnt[k], waits)
            for i, v in enumerate(self.dcnt):
                if v > 0:
                    self._need(e, i, v, waits)
            self._emit_waits(e, waits)

    def finish(self, e, bufs):
        waits = {}
        for b in bufs:
            if b.w is not None:
                self._need(e, b.w[0], b.w[1], waits)
        self._emit_waits(e, waits)

    def emit(self):
        nc = self.nc
        rank = {}
        for k in self.ENG:
            rank[k] = {v: i + 1 for i, v in enumerate(sorted(self.flag[k]))}

        def run(e, eng):
            for rec in self.prog[e]:
                if rec[0] == "w":
                    key, val = rec[1], rec[2]
                    if isinstance(key, str):
                        eng.wait_ge(self.sem[key], rank[key][val])
                    else:
                        eng.wait_ge(self.dsem[key], val)
                elif rec[0] == "o":
                    ins = rec[1](eng)
                    if rec[2] in rank[e]:
                        ins.then_inc(self.sem[e], 1)
                else:
                    rec[1](eng).then_inc(self.dsem[rec[2]], 16)

        with nc.Block() as block:
            @block.sync
            def _(eng):
                run("sp", eng)

            @block.tensor
            def _(eng):
                run("pe", eng)

            @block.scalar
            def _(eng):
                run("act", eng)

            @block.vector
            def _(eng):
                run("dve", eng)

            @block.gpsimd
            def _(eng):
                run("pool", eng)


class Rot:
    def __init__(self, tensors):
        self.items = [(t, Buf()) for t in tensors]
        self.i = 0

    def get(self):
        it = self.items[self.i]
        self.i = (self.i + 1) % len(self.items)
        return it


class StopBuild(Exception):
    pass


STOP = [None]
ATT_PIPE = True
import os
DENG = os.environ.get("DENG", "pool,pool,pool").split(",")


def build_nc(NT=NT):
    try:
        return _build_nc(NT)
    except StopBuild as ex:
        return ex.args[0]


def _build_nc(NT=NT):
    nc = bass.Bass("TRN2", target_bir_lowering=False)
    dt_in = lambda name, shape: nc.dram_tensor(name, shape, F32, kind="ExternalInput").ap()
    x_d = dt_in("x", [SEQ, D])
    cT_d = dt_in("cT", [128, 8])
    wada_d = dt_in("w_ada", [D, 6 * D])
    win_d = dt_in("w_in", [D, DIN])
    wao_d = dt_in("w_attn_o", [512, D])
    wco_d = dt_in("w_conv_o", [512, D])
    wmo_d = dt_in("w_mix_o", [D, D])
    wup_d = dt_in("w_up", [D, 2 * DFF])
    wdn_d = dt_in("w_down", [DFF, D])
    colv_d = dt_in("colv", [128, NCOLV])
    rowv_d = dt_in("rowv", [128, NROWV])
    biasT_d = dt_in("biasT", [128, 2 * 8 * 128])
    constb_d = dt_in("constb", [128, 8])
    out_d = nc.dram_tensor("out", [SEQ, D], F32, kind="ExternalOutput").ap()
    x1_d = nc.dram_tensor("x1s", [SEQ, D], F32, kind="Internal").ap()
    gf_d = nc.dram_tensor("gfs", [128, D], F32, kind="Internal").ap()
    bh_d = nc.dram_tensor("bhs", [128, 2048], BF16, kind="Internal").ap()
    bl_d = nc.dram_tensor("bls", [128, 2048], BF16, kind="Internal").ap()

    b_x1 = [Buf() for _ in range(SEQ // 128)]
    b_out = [Buf() for _ in range(SEQ // 128)]
    s_bufs = [Buf("S0"), Buf("S1")]

    with ExitStack() as es:
        S = Sched(nc, es)

        def ck(n):
            if STOP[0] == n:
                S.barrier()
                S.emit()
                global LAST_SCHED
                LAST_SCHED = S
                raise StopBuild(nc)
        sb = lambda st, name, shape, dt: st.enter_context(nc.sbuf_tensor("sb_" + name, shape, dt))

        colv = sb(es, "colv", [128, NCOLV], F32); b_colv = Buf()
        dv = sb(es, "dv", [128, 160], F32); b_dv = Buf()
        modc = sb(es, "modc", [128, 32], F32); b_modc = Buf()
        identb = sb(es, "identb", [128, 128], BF16); b_ident = Buf()
        onesf = sb(es, "onesf", [128, 128], F32); b_onesf = Buf()
        GP = sb(es, "GP", [128, D], F32); b_GP = Buf()
        b_GF = Buf(); b_gfd = Buf(); b_bhd = Buf(); b_bld = Buf()
        junk = sb(es, "junk", [128, D], BF16); b_junk = Buf()
        small = sb(es, "small", [128, 64], F32)
        small_rot = Rot([small[:, i:i + 1] for i in range(64)])
        psum = es.enter_context(nc.psum_tensor("ps_all", [128, 4096], F32))
        bk = [Buf("bank%d" % i) for i in range(8)]
        bank = lambda i: psum[:, i * 512:(i + 1) * 512]
        V_QB8, V_HGLB, V_HGAB, V_WDWH, V_GLNH, V_BLNH = 0, 4, 8, 24, 148, 152
        A1, B1, A2, B2 = 0, 8, 16, 24

        S.dma("sp", lambda e: e.dma_start(out=colv[:], in_=colv_d), writes=[b_colv])

        S.op("pool", lambda e: e.memset(onesf[:], 0.0), writes=[b_onesf])
        S.op("pool", lambda e: e.affine_select(out=onesf[:], in_=onesf[:], pattern=[[-1, 128]],
                                                compare_op=ALU.not_equal, fill=1.0, base=0,
                                                channel_multiplier=1),
             reads=[b_onesf], writes=[b_onesf])
        S.op("dve", lambda e: e.tensor_copy(out=identb[:], in_=onesf[:]), reads=[b_onesf], writes=[b_ident])
        S.op("pool", lambda e: e.memset(onesf[:], 1.0), reads=[b_onesf], writes=[b_onesf])

        def dcol(dst, n, src, mul):
            S.op("dve", lambda e: e.tensor_scalar(out=dv[:, dst:dst + n], in0=colv[:, src:src + n],
                                                  scalar1=mul, scalar2=None, op0=ALU.mult),
                 reads=[b_colv], writes=[b_dv])
        dcol(V_QB8, 4, O_BIN + 0, 0.125)
        dcol(V_HGLB, 4, O_BIN + 16, 0.5)
        dcol(V_HGAB, 16, O_BIN + 20, 0.5)
        dcol(V_WDWH, 124, O_WDW, 0.5)
        dcol(V_GLNH, 4, O_GLN, 0.5)
        dcol(V_BLNH, 4, O_BLN, 0.5)

        ck(1)
        with ExitStack() as s0:
            cT = sb(s0, "cT", [128, 8], F32); b_cT = Buf()
            cth = sb(s0, "cth", [128, 8], F32); b_cth = Buf()
            cact = sb(s0, "cact", [128, 8], F32); b_cact = Buf()
            cactb = sb(s0, "cactb", [128, 8], BF16); b_cactb = Buf()
            onesb = sb(s0, "onesb", [128, 128], BF16); b_onesb = Buf()
            crep = sb(s0, "crep", [128, 8, 128], BF16); b_crep = Buf()
            wa_rot = Rot([sb(s0, "wa%d" % i, [128, 8, 1024], BF16) for i in range(2)])
            rrow = Rot([sb(s0, "rr%d" % i, [128, 1024], F32) for i in range(2)])
            rtmp = sb(s0, "rtmp", [128, 1024], F32); b_rtmp = Buf()
            GFs = sb(s0, "GFs", [128, D], F32)

            S.dma("sp", lambda e: e.dma_start(out=cT[:], in_=cT_d), writes=[b_cT])
            S.op("act", lambda e: e.activation(out=cth[:], in_=cT[:], func=AF.Tanh, scale=0.5),
                 reads=[b_cT], writes=[b_cth])
            S.op("dve", lambda e: e.scalar_tensor_tensor(out=cact[:], in0=cth[:], scalar=1.0, in1=cT[:],
                                                         op0=ALU.add, op1=ALU.mult),
                 reads=[b_cth, b_cT], writes=[b_cact])
            S.op("dve", lambda e: e.tensor_scalar(out=cact[:], in0=cact[:], scalar1=0.5, scalar2=None,
                                                  op0=ALU.mult), reads=[b_cact], writes=[b_cact])
            S.op("dve", lambda e: e.tensor_copy(out=cactb[:], in_=cact[:]), reads=[b_cact], writes=[b_cactb])
            S.op("pool", lambda e: e.memset(onesb[:], 1.0), writes=[b_onesb])
            for dc in range(8):
                S.op("dve", lambda e, dc=dc: e.tensor_scalar(out=crep[:, dc, :], in0=onesb[:],
                                                             scalar1=cact[:, dc:dc + 1], scalar2=None,
                                                             op0=ALU.mult),
                     reads=[b_onesb, b_cact], writes=[b_crep])
            wada_v = wada_d.rearrange("(dc p) n -> p dc n", p=128)
            for g in range(6):
                wa, b_wa = wa_rot.get()
                S.dma("pool", lambda e, wa=wa, g=g: e.dma_start(out=wa[:], in_=wada_v[:, :, g * 1024:(g + 1) * 1024]),
                      writes=[b_wa])
                if g in (2, 5):
                    for half in range(2):
                        pb = bank(half)
                        for dc in range(8):
                            S.op("pe", lambda e, pb=pb, wa=wa, dc=dc, half=half: e.matmul(
                                pb, lhsT=crep[:, dc, :], rhs=wa[:, dc, half * 512:(half + 1) * 512],
                                start=(dc == 0), stop=(dc == 7)),
                                reads=[b_crep, b_wa], writes=[bk[half]])
                    r1, b_r1 = rrow.get()
                    r2, b_r2 = rrow.get()
                    o1 = R_BGTM if g == 2 else R_BGTF
                    o2 = R_GPOSTM if g == 2 else R_GPOSTF
                    S.dma("sp", lambda e, r1=r1, o1=o1: e.dma_start(out=r1[:], in_=rowv_d[:, o1:o1 + 1024]), writes=[b_r1])
                    S.dma("sp", lambda e, r2=r2, o2=o2: e.dma_start(out=r2[:], in_=rowv_d[:, o2:o2 + 1024]), writes=[b_r2])
                    G = GP if g == 2 else GFs
                    b_G = b_GP if g == 2 else b_GF
                    S.op("dve", lambda e, r1=r1: e.tensor_tensor(out=rtmp[:], in0=psum[:, 0:1024], in1=r1[:], op=ALU.add),
                         reads=[bk[0], bk[1], b_r1], writes=[b_rtmp])
                    S.op("dve", lambda e, r2=r2, G=G: e.tensor_tensor(out=G[:], in0=rtmp[:], in1=r2[:], op=ALU.mult),
                         reads=[b_rtmp, b_r2], writes=[b_G])
                else:
                    for oc in range(8):
                        col = 1024 + g * 8 + oc
                        for dc in range(8):
                            S.op("pe", lambda e, wa=wa, oc=oc, dc=dc, col=col: e.matmul(
                                psum[:, col:col + 1], lhsT=wa[:, dc, oc * 128:(oc + 1) * 128],
                                rhs=cactb[:, dc:dc + 1], start=(dc == 0), stop=(dc == 7)),
                                reads=[b_wa, b_cactb], writes=[bk[2]])
            mt_ = sb(s0, "modT", [128, 48], F32); b_mt = Buf()
            for c0 in (0, 24):
                S.op("dve", lambda e, c0=c0: e.tensor_tensor(out=mt_[:, c0:c0 + 16], in0=psum[:, 1024 + c0:1040 + c0],
                                                             in1=colv[:, O_BADA + c0:O_BADA + c0 + 16], op=ALU.add),
                     reads=[bk[2], b_colv], writes=[b_mt])
            S.op("dve", lambda e: e.scalar_tensor_tensor(out=modc[:, A1:A1 + 8], in0=mt_[:, 8:16], scalar=1.0,
                                                         in1=colv[:, O_GPM:O_GPM + 8], op0=ALU.add, op1=ALU.mult),
                 reads=[b_mt, b_colv], writes=[b_modc])
            S.op("dve", lambda e: e.tensor_copy(out=modc[:, B1:B1 + 8], in_=mt_[:, 0:8]), reads=[b_mt], writes=[b_modc])
            S.op("dve", lambda e: e.scalar_tensor_tensor(out=modc[:, A2:A2 + 8], in0=mt_[:, 32:40], scalar=1.0,
                                                         in1=colv[:, O_GPF:O_GPF + 8], op0=ALU.add, op1=ALU.mult),
                 reads=[b_mt, b_colv], writes=[b_modc])
            S.op("dve", lambda e: e.tensor_copy(out=modc[:, B2:B2 + 8], in_=mt_[:, 24:32]), reads=[b_mt], writes=[b_modc])
            S.dma("sp", lambda e: e.dma_start(out=gf_d, in_=GFs[:]), reads=[b_GF], writes=[b_gfd])
            bt32 = sb(s0, "bt32", [128, 2048], F32); b_bt32 = Buf()
            bhi = sb(s0, "bhi", [128, 2048], BF16); b_bhi = Buf()
            bhf = sb(s0, "bhf", [128, 2048], F32); b_bhf = Buf()
            blo = sb(s0, "blo", [128, 2048], BF16); b_blo = Buf()
            S.dma("sp", lambda e: e.dma_start(out=bt32[:], in_=biasT_d), writes=[b_bt32])
            S.op("dve", lambda e: e.tensor_copy(out=bhi[:], in_=bt32[:]), reads=[b_bt32], writes=[b_bhi])
            S.op("dve", lambda e: e.tensor_copy(out=bhf[:], in_=bhi[:]), reads=[b_bhi], writes=[b_bhf])
            S.op("dve", lambda e: e.tensor_tensor(out=blo[:], in0=bt32[:], in1=bhf[:], op=ALU.subtract),
                 reads=[b_bt32, b_bhf], writes=[b_blo])
            S.dma("sp", lambda e: e.dma_start(out=bh_d, in_=bhi[:]), reads=[b_bhi], writes=[b_bhd])
            S.dma("sp", lambda e: e.dma_start(out=bl_d, in_=blo[:]), reads=[b_blo], writes=[b_bld])
            S.barrier()

        ck(2)
        tp_bufs = [Buf("tp0"), Buf("tp1")]
        tpv = psum[:, 7 * 512:8 * 512].bitcast(BF16)
        tp_rot_i = [0]

        def prenorm(src_d, j, xin, b_xin, xn_rot, hT, b_hT, s, Acol, Bcol, split=False):
            ss, b_ss = small_rot.get()
            S.op("act", lambda e: e.activation(out=junk[:], in_=xin[:], func=AF.Square, accum_out=ss),
                 reads=[b_xin], writes=[b_junk, b_ss])
            r1, b_r1 = small_rot.get()
            S.op("dve", lambda e: e.tensor_scalar(out=r1, in0=ss, scalar1=1.0 / D, scalar2=EPS,
                                                  op0=ALU.mult, op1=ALU.add), reads=[b_ss], writes=[b_r1])
            r2, b_r2 = small_rot.get()
            S.op("act", lambda e: e.activation(out=r2, in_=r1, func=AF.Sqrt), reads=[b_r1], writes=[b_r2])
            S.op("dve", lambda e: e.reciprocal(out=r2, in_=r2), reads=[b_r2], writes=[b_r2])
            xn, b_xn = xn_rot.get()
            S.op("dve", lambda e: e.tensor_scalar(out=xn[:], in0=xin[:], scalar1=r2, scalar2=None, op0=ALU.mult),
                 reads=[b_xin, b_r2], writes=[b_xn])
            if split:
                return xn, b_xn
            prenorm_b(xn, b_xn, hT, b_hT, s, Acol, Bcol)

        def prenorm_b(xn, b_xn, hT, b_hT, s, Acol, Bcol, tb=7):
            tpv = psum[:, tb * 512:(tb + 1) * 512].bitcast(BF16)
            for dc in range(8):
                S.op("pe", lambda e, dc=dc, xn=xn: e.transpose(
                    tpv[:, dc * 128:(dc + 1) * 128], xn[:, dc * 128:(dc + 1) * 128], identb[:]),
                    reads=[b_xn, b_ident], writes=[bk[tb]])
            for dc in range(8):
                if dc % 2 == 0:
                    S.op("act", lambda e, dc=dc: e.activation(
                        out=hT[:, dc, s * 128:(s + 1) * 128], in_=tpv[:, dc * 128:(dc + 1) * 128],
                        func=AF.Identity, scale=modc[:, Acol + dc:Acol + dc + 1],
                        bias=modc[:, Bcol + dc:Bcol + dc + 1]),
                        reads=[bk[tb], b_modc], writes=[b_hT])
                else:
                    S.op("dve", lambda e, dc=dc: e.tensor_scalar(
                        out=hT[:, dc, s * 128:(s + 1) * 128], in0=tpv[:, dc * 128:(dc + 1) * 128],
                        scalar1=modc[:, Acol + dc:Acol + dc + 1], scalar2=modc[:, Bcol + dc:Bcol + dc + 1],
                        op0=ALU.mult, op1=ALU.add),
                        reads=[bk[tb], b_modc], writes=[b_hT])

        with ExitStack() as p1:
            win = sb(p1, "win", [128, 8, DIN], BF16)
            b_win = [Buf() for _ in range(4)]
            wao = sb(p1, "wao", [128, 4, D], BF16); b_wao = Buf()
            wco = sb(p1, "wco", [128, 4, D], BF16); b_wco = Buf()
            wmo = sb(p1, "wmo", [128, 8, D], BF16); b_wmo = Buf()
            Bh = sb(p1, "Bh", [128, 2, 8, 128], BF16); b_Bh = Buf()
            Bl = sb(p1, "Bl", [128, 2, 8, 128], BF16); b_Bl = Buf()
            Lm = sb(p1, "Lm", [1, 128], BF16); Rm = sb(p1, "Rm", [1, 128], BF16); b_LR = Buf()
            S.op("pool", lambda e: e.memset(Lm[:, 0:64], 1.0), writes=[b_LR])
            S.op("pool", lambda e: e.memset(Lm[:, 64:128], 0.0), writes=[b_LR])
            S.op("pool", lambda e: e.memset(Rm[:, 0:64], 0.0), writes=[b_LR])
            S.op("pool", lambda e: e.memset(Rm[:, 64:128], NEG), writes=[b_LR])
            constb = sb(p1, "constb", [128, 8], F32); b_cb = Buf()
            bvb = sb(p1, "bvb", [128, 512], F32); b_bvb = Buf()
            xin_rot = Rot([sb(p1, "xin%d" % i, [128, D], F32) for i in range(2)])
            xres_rot = Rot([sb(p1, "xres%d" % i, [128, D], F32) for i in range(1)])
            xn_rot = Rot([sb(p1, "xn%d" % i, [128, D], BF16) for i in range(1)])
            hT2 = [(sb(p1, "hT%d" % i, [128, 8, T], BF16), Buf()) for i in range(2)]
            QT2 = [(sb(p1, "QT%d" % i, [128, 4, T], BF16), Buf()) for i in range(2)]
            KT = sb(p1, "KT", [128, 4, 1024], BF16)
            b_KT = [Buf() for _ in range(4)]
            Vr = sb(p1, "Vr", [128, 8, 8, 65], BF16)
            b_Vr = [Buf() for _ in range(8)]
            u2 = [(sb(p1, "u%d" % i, [128, 4, 30 + T], BF16), [Buf() for _ in range(4)]) for i in range(2)]
            cacc = sb(p1, "cacc", [128, 4, T], F32)
            b_cacc = [Buf() for _ in range(4)]
            stA = sb(p1, "stA", [128, T], F32); b_stA = Buf()
            stB = sb(p1, "stB", [128, T], F32); b_stB = Buf()
            uT2 = [(sb(p1, "uT%d" % i, [128, 4, T], BF16), Buf()) for i in range(2)]
            f_rot = Rot([sb(p1, "f%d" % i, [128, T], F32) for i in range(9)])
            PT_rot = Rot([sb(p1, "PT%d" % i, [128, 5, 128], BF16) for i in range(2)])
            rc_t = sb(p1, "rc", [128, 16], F32)
            rc_rot = Rot([rc_t[:, i * 4:(i + 1) * 4] for i in range(4)])
            attn_rot = Rot([sb(p1, "attn%d" % i, [128, 512], BF16) for i in range(1)])
            attnT2 = [(sb(p1, "attnT%d" % i, [128, 4, T], BF16), Buf()) for i in range(2)]
            tg_rot = Rot([sb(p1, "tg%d" % i, [128, 2, T], F32) for i in range(1)])
            yT2 = [(sb(p1, "yT%d" % i, [128, 8, T], BF16), Buf()) for i in range(2)]

            win_v = win_d.rearrange("(dc p) n -> p dc n", p=128)
            groups = [(0, 1536), (1536, 2560), (2560, 3584), (3584, 4608)]
            for gi, (c0, c1) in enumerate(groups):
                S.dma("pool", lambda e, c0=c0, c1=c1: e.dma_start(out=win[:, :, c0:c1], in_=win_v[:, :, c0:c1]),
                      writes=[b_win[gi]])
            S.dma("sp", lambda e: e.dma_start(out=Bh[:].rearrange("p a h q -> p (a h q)"), in_=bh_d), reads=[b_bhd], writes=[b_Bh])
            S.dma("sp", lambda e: e.dma_start(out=Bl[:].rearrange("p a h q -> p (a h q)"), in_=bl_d), reads=[b_bld], writes=[b_Bl])
            S.dma("sp", lambda e: e.dma_start(out=constb[:], in_=constb_d), writes=[b_cb])
            S.dma("sp", lambda e: e.dma_start(out=bvb[:], in_=rowv_d[:, R_BV:R_BV + 512]), writes=[b_bvb])
            S.dma("pool", lambda e: e.dma_start(out=wao[:], in_=wao_d.rearrange("(c p) n -> p c n", p=128)), writes=[b_wao])
            S.dma("pool", lambda e: e.dma_start(out=wco[:], in_=wco_d.rearrange("(c p) n -> p c n", p=128)), writes=[b_wco])
            S.dma("pool", lambda e: e.dma_start(out=wmo[:], in_=wmo_d.rearrange("(c p) n -> p c n", p=128)), writes=[b_wmo])
            S.op("pool", lambda e: e.memset(Vr[:].rearrange("p a h d -> p (a h d)"), 1.0), writes=b_Vr)
            S.op("pool", lambda e: e.memset(u2[0][0][:, :, 0:30], 0.0), writes=u2[0][1])

            ck(3)
            mm_i = [0]

            def mmbank():
                i = mm_i[0]
                mm_i[0] ^= 1
                return bank(i), bk[i]

            def load_x(j):
                xin, b_xin = xin_rot.get()
                S.dma("sp", lambda e, xin=xin, j=j: e.dma_start(out=xin[:], in_=x_d[j * 128:(j + 1) * 128, :]),
                      writes=[b_xin])
                return xin, b_xin

            def tile_gen(mt):
                par = mt % 2
                hT, b_hT = hT2[par]
                QT, b_QT = QT2[par]
                u, b_u = u2[par]
                u_o, b_uo = u2[1 - par]
                uT, b_uT = uT2[par]
                attnT, b_attnT = attnT2[par]
                yT, b_yT = yT2[par]
                cur = [load_x(2 * mt), load_x(2 * mt + 1)]
                ring = mt % 4
                for s in range(2):
                    xa = prenorm(x_d, 2 * mt + s, cur[s][0], cur[s][1], xn_rot, hT, b_hT, s, A1, B1, split=True)
                    yield 0
                    prenorm_b(xa[0], xa[1], hT, b_hT, s, A1, B1)
                    yield 0
                def qk_group(oc):
                    pb, bb = mmbank()
                    for dc in range(8):
                        S.op("pe", lambda e, pb=pb, oc=oc, dc=dc: e.matmul(
                            pb[:, 0:T], lhsT=win[:, dc, oc * 128:(oc + 1) * 128], rhs=hT[:, dc, :],
                            start=(dc == 0), stop=(dc == 7)), reads=[b_win[0], b_hT], writes=[bb])
                    if oc < 4:
                        S.op("act", lambda e, pb=pb, oc=oc: e.activation(
                            out=QT[:, oc, :], in_=pb[:, 0:T], func=AF.Identity, scale=0.125,
                            bias=dv[:, V_QB8 + oc:V_QB8 + oc + 1]), reads=[bb, b_dv], writes=[b_QT])
                    else:
                        S.op("act", lambda e, pb=pb, oc=oc, ring=ring: e.activation(
                            out=KT[:, oc - 4, ring * T:(ring + 1) * T], in_=pb[:, 0:T], func=AF.Identity,
                            bias=colv[:, O_BIN + oc:O_BIN + oc + 1]),
                            reads=[bb, b_colv], writes=[b_KT[ring]])

                def v_group(s):
                    j = 2 * mt + s
                    pb, bb = mmbank()
                    for dc in range(8):
                        S.op("pe", lambda e, pb=pb, dc=dc, s=s: e.matmul(
                            pb, lhsT=hT[:, dc, s * 128:(s + 1) * 128], rhs=win[:, dc, 1024:1536],
                            start=(dc == 0), stop=(dc == 7)), reads=[b_win[0], b_hT], writes=[bb])
                    S.op("dve", lambda e, pb=pb, j=j: e.tensor_tensor(
                        out=Vr[:, j % 8, :, 0:64], in0=pb.rearrange("p (h d) -> p h d", h=8),
                        in1=bvb[:].rearrange("p (h d) -> p h d", h=8), op=ALU.add),
                        reads=[bb, b_bvb], writes=[b_Vr[j % 8]])

                deferred = [(lambda oc=oc: qk_group(oc)) for oc in range(8)] + [(lambda s=s: v_group(s)) for s in range(2)]
                for cc in range(4):
                    pb, bb = mmbank()
                    for half, base in ((0, 1536), (1, 2048)):
                        for dc in range(8):
                            S.op("pe", lambda e, pb=pb, cc=cc, dc=dc, half=half, base=base: e.matmul(
                                pb[:, half * T:(half + 1) * T],
                                lhsT=win[:, dc, base + cc * 128:base + (cc + 1) * 128], rhs=hT[:, dc, :],
                                start=(dc == 0), stop=(dc == 7)), reads=[b_win[1], b_hT], writes=[bb])
                    tgl, b_tgl = f_rot.get()
                    S.op("act", lambda e, pb=pb, cc=cc, tgl=tgl: e.activation(
                        out=tgl[:], in_=pb[:, T:2 * T], func=AF.Tanh, scale=0.5,
                        bias=dv[:, V_HGLB + cc:V_HGLB + cc + 1]), reads=[bb, b_dv], writes=[b_tgl])
                    ab, b_ab = f_rot.get()
                    S.op("act", lambda e, pb=pb, cc=cc, ab=ab: e.activation(
                        out=ab[:], in_=pb[:, 0:T], func=AF.Identity,
                        bias=colv[:, O_BIN + 12 + cc:O_BIN + 13 + cc]), reads=[bb, b_colv], writes=[b_ab])
                    S.op("dve", lambda e, cc=cc, tgl=tgl, ab=ab: e.scalar_tensor_tensor(
                        out=u[:, cc, 30:30 + T], in0=tgl[:], scalar=1.0, in1=ab[:], op0=ALU.add, op1=ALU.mult),
                        reads=[b_tgl, b_ab], writes=[b_u[cc]])
                    yield 0
                yield 1
                for k in range(31):
                    for cc in range(4):
                        eng = "dve"
                        wcol = dv[:, V_WDWH + k * 4 + cc:V_WDWH + k * 4 + cc + 1]
                        if k == 0:
                            S.op(eng, lambda e, cc=cc, wcol=wcol: e.tensor_scalar(
                                out=cacc[:, cc, :], in0=u[:, cc, 0:T], scalar1=wcol,
                                scalar2=colv[:, O_BDW + cc:O_BDW + cc + 1], op0=ALU.mult, op1=ALU.add),
                                reads=[b_u[cc], b_dv, b_colv], writes=[b_cacc[cc]])
                        else:
                            S.op(eng, lambda e, cc=cc, k=k, wcol=wcol: e.scalar_tensor_tensor(
                                out=cacc[:, cc, :], in0=u[:, cc, k:k + T], scalar=wcol, in1=cacc[:, cc, :],
                                op0=ALU.mult, op1=ALU.add),
                                reads=[b_u[cc], b_dv, b_cacc[cc]], writes=[b_cacc[cc]])
                    if k % 3 == 1 and deferred:
                        deferred.pop(0)()
                    yield 1
                while deferred:
                    deferred.pop(0)()
                for cc in range(4):
                    eng = "dve" if cc < 2 else "pool"
                    S.op(eng, lambda e, cc=cc: e.tensor_copy(out=u_o[:, cc, 0:30], in_=u[:, cc, T:T + 30]),
                         reads=[b_u[cc]], writes=[b_uo[cc]])
                pb, bb = mmbank()
                for cc in range(4):
                    S.op("pe", lambda e, pb=pb, cc=cc: e.matmul(pb[:, 0:T], lhsT=onesf[:], rhs=cacc[:, cc, :],
                                                                 start=(cc == 0), stop=(cc == 3)),
                         reads=[b_onesf, b_cacc[cc]], writes=[bb])
                for cc in range(4):
                    sq, b_sq = f_rot.get()
                    S.op("act", lambda e, cc=cc, sq=sq: e.activation(out=sq[:], in_=cacc[:, cc, :], func=AF.Square),
                         reads=[b_cacc[cc]], writes=[b_sq])
                    S.op("pe", lambda e, pb=pb, cc=cc, sq=sq: e.matmul(pb[:, T:2 * T], lhsT=onesf[:], rhs=sq[:],
                                                                        start=(cc == 0), stop=(cc == 3)),
                         reads=[b_onesf, b_sq], writes=[bb])
                mean, b_mean = f_rot.get()
                S.op("dve", lambda e, pb=pb, mean=mean: e.tensor_scalar(out=mean[:], in0=pb[:, 0:T], scalar1=1.0 / 512,
                                                                        scalar2=None, op0=ALU.mult),
                     reads=[bb], writes=[b_mean])
                var, b_var = f_rot.get()
                S.op("dve", lambda e, mean=mean, var=var: e.tensor_tensor(out=var[:], in0=mean[:], in1=mean[:], op=ALU.mult),
                     reads=[b_mean], writes=[b_var])
                S.op("dve", lambda e, pb=pb, var=var: e.scalar_tensor_tensor(
                    out=var[:], in0=pb[:, T:2 * T], scalar=1.0 / 512, in1=var[:], op0=ALU.mult, op1=ALU.subtract),
                    reads=[bb, b_var], writes=[b_var])
                S.op("dve", lambda e, var=var: e.tensor_scalar(out=var[:], in0=var[:], scalar1=EPS, scalar2=None,
                                                                op0=ALU.add), reads=[b_var], writes=[b_var])
                S.op("act", lambda e, var=var: e.activation(out=stA[:], in_=var[:], func=AF.Sqrt),
                     reads=[b_var], writes=[b_stA])
                S.op("dve", lambda e: e.reciprocal(out=stA[:], in_=stA[:]), reads=[b_stA], writes=[b_stA])
                S.op("dve", lambda e, mean=mean: e.scalar_tensor_tensor(
                    out=stB[:], in0=mean[:], scalar=-1.0, in1=stA[:], op0=ALU.mult, op1=ALU.mult),
                    reads=[b_mean, b_stA], writes=[b_stB])
                yield 1
                for cc in range(4):
                    eng = "dve" if cc < 2 else "pool"
                    S.op(eng, lambda e, cc=cc: e.tensor_tensor(out=cacc[:, cc, :], in0=cacc[:, cc, :], in1=stA[:], op=ALU.mult),
                         reads=[b_cacc[cc], b_stA], writes=[b_cacc[cc]])
                    S.op(eng, lambda e, cc=cc: e.tensor_tensor(out=cacc[:, cc, :], in0=cacc[:, cc, :], in1=stB[:], op=ALU.add),
                         reads=[b_cacc[cc], b_stB], writes=[b_cacc[cc]])
                    yh, b_yh = f_rot.get()
                    S.op("act", lambda e, cc=cc, yh=yh: e.activation(
                        out=yh[:], in_=cacc[:, cc, :], func=AF.Identity, scale=dv[:, V_GLNH + cc:V_GLNH + cc + 1],
                        bias=dv[:, V_BLNH + cc:V_BLNH + cc + 1]), reads=[b_cacc[cc], b_dv], writes=[b_yh])
                    th, b_th = f_rot.get()
                    S.op("act", lambda e, yh=yh, th=th: e.activation(out=th[:], in_=yh[:], func=AF.Tanh),
                         reads=[b_yh], writes=[b_th])
                    S.op("dve", lambda e, cc=cc, yh=yh, th=th: e.scalar_tensor_tensor(
                        out=uT[:, cc, :], in0=th[:], scalar=1.0, in1=yh[:], op0=ALU.add, op1=ALU.mult),
                        reads=[b_th, b_yh], writes=[b_uT])
                    yield 1
                yield 2
                for s in range(2):
                    j = 2 * mt + s
                    i0 = max(0, 4 - j)
                    at, b_at = attn_rot.get()
                    pv = psum[:, 6 * 512:6 * 512 + 260].rearrange("p (h d) -> p h d", d=65)

                    def emit_S(h, s=s, j=j, i0=i0):
                        hp, po = h // 2, (h % 2) * 64
                        so = 1024 + (h % 2) * 1024
                        for i in range(i0, 5):
                            kb = j - 4 + i
                            rs, ro = (kb // 2) % 4, (kb % 2) * 128
                            S.op("pe", lambda e, so=so, i=i, po=po, hp=hp, rs=rs, ro=ro, s=s: e.matmul(
                                psum[:, so + i * 128:so + (i + 1) * 128],
                                lhsT=KT[po:po + 64, hp, rs * T + ro:rs * T + ro + 128],
                                rhs=QT[po:po + 64, hp, s * 128:(s + 1) * 128], start=True, stop=(0 < i < 3 or (i == 0 and i0 > 0))),
                                reads=[b_KT[rs], b_QT], writes=[s_bufs[h % 2]])
                            if i == 0 and i0 == 0:
                                S.op("pe", lambda e, so=so: e.matmul(
                                    psum[:, so:so + 128], lhsT=Lm[:], rhs=Rm[:], start=False, stop=True),
                                    reads=[b_LR], writes=[s_bufs[h % 2]])
                            if i >= 3:
                                S.op("pe", lambda e, so=so, i=i, h=h: e.matmul(
                                    psum[:, so + i * 128:so + (i + 1) * 128], lhsT=identb[:], rhs=Bh[:, i - 3, h, :],
                                    start=False, stop=False), reads=[b_ident, b_Bh], writes=[s_bufs[h % 2]])
                                S.op("pe", lambda e, so=so, i=i, h=h: e.matmul(
                                    psum[:, so + i * 128:so + (i + 1) * 128], lhsT=identb[:], rhs=Bl[:, i - 3, h, :],
                                    start=False, stop=True), reads=[b_ident, b_Bl], writes=[s_bufs[h % 2]])

                    def emit_soft(h, i0=i0):
                        so = 1024 + (h % 2) * 1024
                        bS = s_bufs[h % 2]
                        PT, b_PT = PT_rot.get()
                        if i0 < 3:
                            S.op("act", lambda e, so=so, i0=i0, PT=PT, h=h: e.activation(
                                out=PT[:, i0:3, :],
                                in_=psum[:, so + i0 * 128:so + 3 * 128].rearrange("p (a q) -> p a q", q=128),
                                func=AF.Exp, bias=constb[:, h:h + 1]), reads=[bS, b_cb], writes=[b_PT])
                        t0 = max(i0, 3)
                        S.op("act", lambda e, so=so, t0=t0, PT=PT: e.activation(
                            out=PT[:, t0:5, :],
                            in_=psum[:, so + t0 * 128:so + 5 * 128].rearrange("p (a q) -> p a q", q=128),
                            func=AF.Exp), reads=[bS, b_PT], writes=[b_PT])
                        return PT, b_PT

                    def emit_PV(h, PT, b_PT, j=j, i0=i0, pv=pv):
                        for i in range(i0, 5):
                            kb = j - 4 + i
                            S.op("pe", lambda e, pv=pv, PT=PT, i=i, kb=kb, h=h, i0=i0: e.matmul(
                                pv[:, h % 4, :], lhsT=PT[:, i, :], rhs=Vr[:, kb % 8, h, :],
                                start=(i == i0), stop=(i == 4)),
                                reads=[b_PT, b_Vr[kb % 8]], writes=[bk[6]])

                    if ATT_PIPE:
                        emit_S(0)
                    for h in range(8):
                        if not ATT_PIPE:
                            emit_S(h)
                        PT, b_PT = emit_soft(h)
                        if ATT_PIPE and h + 1 < 8:
                            emit_S(h + 1)
                        emit_PV(h, PT, b_PT)
                        if h % 4 == 3:
                            half = h // 4
                            rc, b_rc = rc_rot.get()
                            S.op("dve", lambda e, pv=pv, rc=rc: e.reciprocal(out=rc, in_=pv[:, :, 64]),
                                 reads=[bk[6]], writes=[b_rc])
                            S.op("dve", lambda e, pv=pv, rc=rc, at=at, half=half: e.tensor_tensor(
                                out=at[:, half * 256:(half + 1) * 256].rearrange("p (h d) -> p h d", d=64),
                                in0=pv[:, :, 0:64], in1=rc.unsqueeze(2).to_broadcast([128, 4, 64]), op=ALU.mult),
                                reads=[bk[6], b_rc, b_at], writes=[b_at])
                        yield 2
                    for c4 in range(4):
                        S.op("pe", lambda e, c4=c4, at=at: e.transpose(
                            tpv[:, c4 * 128:(c4 + 1) * 128], at[:, c4 * 128:(c4 + 1) * 128], identb[:]),
                            reads=[b_at, b_ident], writes=[bk[7]])
                    S.op("act", lambda e, s=s: e.activation(
                        out=attnT[:, :, s * 128:(s + 1) * 128], in_=tpv[:, 0:512].rearrange("p (c q) -> p c q", q=128),
                        func=AF.Identity), reads=[bk[7]], writes=[b_attnT])
                    yield 2
                yield 3
                for fc in range(8):
                    pg, bg = mmbank()
                    for half, base, gi in ((0, 2560, 2), (1, 3584, 3)):
                        for dc in range(8):
                            S.op("pe", lambda e, pg=pg, fc=fc, dc=dc, half=half, base=base: e.matmul(
                                pg[:, half * T:(half + 1) * T],
                                lhsT=win[:, dc, base + fc * 128:base + (fc + 1) * 128], rhs=hT[:, dc, :],
                                start=(dc == 0), stop=(dc == 7)), reads=[b_win[gi], b_hT], writes=[bg])
                    pa, ba = mmbank()
                    for c4 in range(4):
                        S.op("pe", lambda e, pa=pa, fc=fc, c4=c4: e.matmul(
                            pa[:, 0:T], lhsT=wao[:, c4, fc * 128:(fc + 1) * 128], rhs=attnT[:, c4, :],
                            start=(c4 == 0), stop=(c4 == 3)), reads=[b_wao, b_attnT], writes=[ba])
                    for c4 in range(4):
                        S.op("pe", lambda e, pa=pa, fc=fc, c4=c4: e.matmul(
                            pa[:, T:2 * T], lhsT=wco[:, c4, fc * 128:(fc + 1) * 128], rhs=uT[:, c4, :],
                            start=(c4 == 0), stop=(c4 == 3)), reads=[b_wco, b_uT], writes=[ba])
                    tg, b_tg = tg_rot.get()
                    for half in range(2):
                        S.op("act", lambda e, pg=pg, tg=tg, half=half, fc=fc: e.activation(
                            out=tg[:, half, :], in_=pg[:, half * T:(half + 1) * T], func=AF.Tanh, scale=0.5,
                            bias=dv[:, V_HGAB + half * 8 + fc:V_HGAB + half * 8 + fc + 1]),
                            reads=[bg, b_dv], writes=[b_tg])
                    asb, b_asb = f_rot.get()
                    S.op("act", lambda e, pa=pa, asb=asb: e.activation(out=asb[:], in_=pa[:, 0:T], func=AF.Identity),
                         reads=[ba], writes=[b_asb])
                    cbb, b_cbb = f_rot.get()
                    S.op("act", lambda e, pa=pa, cbb=cbb, fc=fc: e.activation(
                        out=cbb[:], in_=pa[:, T:2 * T], func=AF.Identity,
                        bias=colv[:, O_BCO + fc:O_BCO + fc + 1]), reads=[ba, b_colv], writes=[b_cbb])
                    t1, b_t1 = f_rot.get()
                    S.op("dve", lambda e, tg=tg, asb=asb, t1=t1: e.scalar_tensor_tensor(
                        out=t1[:], in0=tg[:, 0, :], scalar=1.0, in1=asb[:], op0=ALU.add, op1=ALU.mult),
                        reads=[b_tg, b_asb], writes=[b_t1])
                    t2, b_t2 = f_rot.get()
                    S.op("dve", lambda e, tg=tg, cbb=cbb, t2=t2: e.scalar_tensor_tensor(
                        out=t2[:], in0=tg[:, 1, :], scalar=1.0, in1=cbb[:], op0=ALU.add, op1=ALU.mult),
                        reads=[b_tg, b_cbb], writes=[b_t2])
                    S.op("dve", lambda e, t1=t1, t2=t2, fc=fc: e.tensor_tensor(
                        out=yT[:, fc, :], in0=t1[:], in1=t2[:], op=ALU.add),
                        reads=[b_t1, b_t2], writes=[b_yT])
                    yield 3
                yield 4
                for s in range(2):
                    j = 2 * mt + s
                    xres, b_xres = xres_rot.get()
                    S.dma("sp", lambda e, xres=xres, j=j: e.dma_start(out=xres[:], in_=x_d[j * 128:(j + 1) * 128, :]),
                          writes=[b_xres])
                    pm = psum[:, 2 * 512:4 * 512]
                    for half in range(2):
                        for dc in range(8):
                            S.op("pe", lambda e, pm=pm, half=half, dc=dc, s=s: e.matmul(
                                pm[:, half * 512:(half + 1) * 512], lhsT=yT[:, dc, s * 128:(s + 1) * 128],
                                rhs=wmo[:, dc, half * 512:(half + 1) * 512], start=(dc == 0), stop=(dc == 7)),
                                reads=[b_yT, b_wmo], writes=[s_bufs[0]])
                    ss, b_ss = small_rot.get()
                    S.op("act", lambda e, pm=pm, ss=ss: e.activation(out=junk[:], in_=pm, func=AF.Square, accum_out=ss),
                         reads=[s_bufs[0]], writes=[b_junk, b_ss])
                    r1, b_r1 = small_rot.get()
                    S.op("dve", lambda e, ss=ss, r1=r1: e.tensor_scalar(out=r1, in0=ss, scalar1=1.0 / D, scalar2=4 * EPS,
                                                                        op0=ALU.mult, op1=ALU.add),
                         reads=[b_ss], writes=[b_r1])
                    r2, b_r2 = small_rot.get()
                    S.op("act", lambda e, r1=r1, r2=r2: e.activation(out=r2, in_=r1, func=AF.Sqrt), reads=[b_r1], writes=[b_r2])
                    S.op("dve", lambda e, r2=r2: e.reciprocal(out=r2, in_=r2), reads=[b_r2], writes=[b_r2])
                    S.op("dve", lambda e, pm=pm, r2=r2: e.scalar_tensor_tensor(
                        out=pm, in0=pm, scalar=r2, in1=GP[:], op0=ALU.mult, op1=ALU.mult),
                        reads=[s_bufs[0], b_r2, b_GP], writes=[s_bufs[0]])
                    S.op("dve", lambda e, pm=pm, xres=xres: e.tensor_tensor(out=xres[:], in0=pm, in1=xres[:], op=ALU.add),
                         reads=[s_bufs[0], b_xres], writes=[b_xres])
                    S.dma("sp", lambda e, xres=xres, j=j: e.dma_start(out=x1_d[j * 128:(j + 1) * 128, :], in_=xres[:]),
                          reads=[b_xres], writes=[b_x1[j]])
                    yield 4

            active = []
            st = {}
            next_mt = 0
            while active or next_mt < NT:
                can_start = next_mt < NT and (
                    not active
                    or (len(active) == 1 and st[active[0]] >= 1)
                    or (len(active) == 2 and st[active[0]] >= 4 and st[active[1]] >= 1))
                if can_start:
                    g = tile_gen(next_mt)
                    next_mt += 1
                    active.append(g)
                    st[g] = 0
                for g in list(active):
                    k = active.index(g)
                    if k == 0 or st[g] < st[active[k - 1]]:
                        try:
                            st[g] = next(g)
                        except StopIteration:
                            active.remove(g)
                            del st[g]
            ck(9)
            S.barrier()

        with ExitStack() as p2:
            wup = sb(p2, "wup", [128, 8, 2 * DFF], BF16)
            b_wupq = [[Buf() for _ in range(8)] for _ in range(4)]
            wdn = sb(p2, "wdn", [128, 22, D], BF16)
            b_wdn = [Buf() for _ in range(22)]
            xin_rot = Rot([sb(p2, "x2in%d" % i, [128, D], F32) for i in range(4)])
            xn_rot = Rot([sb(p2, "x2n%d" % i, [128, D], BF16) for i in range(2)])
            h2_rot = Rot([sb(p2, "h2T%d" % i, [128, 8, T], BF16) for i in range(2)])
            us_rot = Rot([sb(p2, "us%d" % i, [128, T + 2], F32) for i in range(8)])
            halo = sb(p2, "halo", [128, 44, 2], F32); b_halo = [Buf() for _ in range(44)]
            f_rot = Rot([sb(p2, "g%d" % i, [128, T], F32) for i in range(15)])
            pr_rot = Rot([sb(p2, "pr%d" % i, [128, T], BF16) for i in range(13)])
            GF = sb(p2, "GF", [128, D], F32); b_GF = Buf()
            S.dma("sp", lambda e: e.dma_start(out=GF[:], in_=gf_d), reads=[b_gfd], writes=[b_GF])

            QCOL = [(0, 1408), (2816, 4224), (1408, 2816), (4224, 5632)]
            for qs, crange in (((0, 1), range(0, 11)), ((2, 3), range(11, 22))):
                for q in qs:
                    c0, c1 = QCOL[q]
                    for dc in range(8):
                        S.dma("pool", lambda e, dc=dc, c0=c0, c1=c1: e.dma_start(
                            out=wup[:, dc, c0:c1], in_=wup_d[dc * 128:(dc + 1) * 128, c0:c1]),
                            writes=[b_wupq[q][dc]])
                for c in crange:
                    S.dma("pool", lambda e, c=c: e.dma_start(out=wdn[:, c, :], in_=wdn_d[c * 128:(c + 1) * 128, :]),
                          writes=[b_wdn[c]])
            S.op("dve", lambda e: e.memset(halo[:].rearrange("p c t -> p (c t)"), 0.0), writes=b_halo)

            ck(10)
            def load_x1(j):
                xin, b_xin = xin_rot.get()
                S.dma("sp", lambda e, xin=xin, j=j: e.dma_start(out=xin[:], in_=x1_d[j * 128:(j + 1) * 128, :]),
                      reads=[b_x1[j]], writes=[b_xin])
                return xin, b_xin

            up_i = [0]
            LAG = 4
            D0 = 6
            prs = {}
            chains = {}

            def up_and_chain(mt, c, h2T, b_h2):
                ub = 4 + up_i[0]
                up_i[0] = (up_i[0] + 1) % 4
                pu = bank(ub)
                res = []
                for half, cch in ((0, c), (1, 22 + c)):
                    for dc in range(8):
                        S.op("pe", lambda e, pu=pu, half=half, cch=cch, dc=dc, h2T=h2T: e.matmul(
                            pu[:, half * T:(half + 1) * T], lhsT=wup[:, dc, cch * 128:(cch + 1) * 128],
                            rhs=h2T[:, dc, :], start=(dc == 0), stop=(dc == 7)),
                            reads=[b_wupq[(0 if cch < 11 else 2 if cch < 22 else 1 if cch < 33 else 3)][dc], b_h2], writes=[bk[ub]])
                for half, cch in ((0, c), (1, 22 + c)):
                    us, b_us = us_rot.get()
                    S.op("act", lambda e, pu=pu, half=half, us=us: e.activation(
                        out=us[:, 2:T + 2], in_=pu[:, half * T:(half + 1) * T], func=AF.Identity),
                        reads=[bk[ub]], writes=[b_us])
                    S.op("act", lambda e, us=us, cch=cch: e.activation(out=us[:, 0:2], in_=halo[:, cch, :], func=AF.Identity),
                         reads=[b_halo[cch]], writes=[b_us])
                    S.op("pool", lambda e, us=us, cch=cch: e.tensor_copy(out=halo[:, cch, :], in_=us[:, T:T + 2]),
                         reads=[b_us], writes=[b_halo[cch]])
                    a, b_a = f_rot.get()
                    w0 = colv[:, O_WFF + 0 * 44 + cch:O_WFF + 0 * 44 + cch + 1]
                    w1 = colv[:, O_WFF + 1 * 44 + cch:O_WFF + 1 * 44 + cch + 1]
                    w2 = colv[:, O_WFF + 2 * 44 + cch:O_WFF + 2 * 44 + cch + 1]
                    bc = colv[:, O_BFF + cch:O_BFF + cch + 1]
                    S.op("pool", lambda e, us=us, a=a, w0=w0, bc=bc: e.tensor_scalar(
                        out=a[:], in0=us[:, 0:T], scalar1=w0, scalar2=bc, op0=ALU.mult, op1=ALU.add),
                        reads=[b_us, b_colv], writes=[b_a])
                    S.op("dve", lambda e, us=us, a=a, w1=w1: e.scalar_tensor_tensor(
                        out=a[:], in0=us[:, 1:T + 1], scalar=w1, in1=a[:], op0=ALU.mult, op1=ALU.add),
                        reads=[b_us, b_colv, b_a], writes=[b_a])
                    S.op("dve", lambda e, us=us, a=a, w2=w2: e.scalar_tensor_tensor(
                        out=a[:], in0=us[:, 2:T + 2], scalar=w2, in1=a[:], op0=ALU.mult, op1=ALU.add),
                        reads=[b_us, b_colv, b_a], writes=[b_a])
                    res.append((a, b_a))
                chains[(mt, c)] = res

            def gelu_prod(mt, c):
                (va, b_va), (ga, b_ga) = chains.pop((mt, c))
                gl, b_gl = f_rot.get()
                S.op("act", lambda e, ga=ga, gl=gl: e.activation(out=gl[:], in_=ga[:], func=AF.Gelu),
                     reads=[b_ga], writes=[b_gl])
                pr, b_pr = pr_rot.get()
                S.op("dve", lambda e, gl=gl, va=va, pr=pr: e.tensor_tensor(out=pr[:], in0=gl[:], in1=va[:], op=ALU.mult),
                     reads=[b_gl, b_va], writes=[b_pr])
                prs[(mt, c)] = (pr, b_pr)

            def down(mt, c):
                pr, b_pr = prs.pop((mt, c))
                for s in range(2):
                    for half in range(2):
                        bi = 2 * s + half
                        S.op("pe", lambda e, bi=bi, s=s, half=half, pr=pr, c=c: e.matmul(
                            bank(bi), lhsT=pr[:, s * 128:(s + 1) * 128], rhs=wdn[:, c, half * 512:(half + 1) * 512],
                            start=(c == 0), stop=(c == 21)), reads=[b_pr, b_wdn[c]], writes=[bk[bi]])

            def postnorm(mt, cur):
                for s in range(2):
                    j = 2 * mt + s
                    x1t, b_x1t = cur[s]
                    pm = psum[:, 2 * s * 512:(2 * s + 2) * 512]
                    ss, b_ss = small_rot.get()
                    S.op("act", lambda e, pm=pm, ss=ss: e.activation(out=junk[:], in_=pm, func=AF.Square, accum_out=ss),
                         reads=[bk[2 * s], bk[2 * s + 1]], writes=[b_junk, b_ss])
                    r1, b_r1 = small_rot.get()
                    S.op("dve", lambda e, ss=ss, r1=r1: e.tensor_scalar(out=r1, in0=ss, scalar1=1.0 / D, scalar2=EPS,
                                                                        op0=ALU.mult, op1=ALU.add),
                         reads=[b_ss], writes=[b_r1])
                    r2, b_r2 = small_rot.get()
                    S.op("act", lambda e, r1=r1, r2=r2: e.activation(out=r2, in_=r1, func=AF.Sqrt), reads=[b_r1], writes=[b_r2])
                    S.op("dve", lambda e, r2=r2: e.reciprocal(out=r2, in_=r2), reads=[b_r2], writes=[b_r2])
                    S.op("dve", lambda e, pm=pm, r2=r2: e.scalar_tensor_tensor(
                        out=pm, in0=pm, scalar=r2, in1=GF[:], op0=ALU.mult, op1=ALU.mult),
                        reads=[bk[2 * s], bk[2 * s + 1], b_r2, b_GF], writes=[bk[2 * s], bk[2 * s + 1]])
                    S.op("dve", lambda e, pm=pm, x1t=x1t: e.tensor_tensor(out=x1t[:], in0=pm, in1=x1t[:], op=ALU.add),
                         reads=[bk[2 * s], bk[2 * s + 1], b_x1t], writes=[b_x1t])
                    S.dma("sp", lambda e, x1t=x1t, j=j: e.dma_start(out=out_d[j * 128:(j + 1) * 128, :], in_=x1t[:]),
                          reads=[b_x1t], writes=[b_out[j]])

            tiles_x = {0: [load_x1(0), load_x1(1)]}
            tiles_h = {0: h2_rot.get()}
            for s in range(2):
                prenorm(x1_d, s, tiles_x[0][s][0], tiles_x[0][s][1], xn_rot, tiles_h[0][0], tiles_h[0][1], s, A2, B2)
            pend = []
            G = 0
            seq = 0
            for mt in range(NT):
                h2T, b_h2 = tiles_h.pop(mt)
                for c in range(22):
                    up_and_chain(mt, c, h2T, b_h2)
                    if c >= 1:
                        gelu_prod(mt, c - 1)
                    elif mt >= 1:
                        gelu_prod(mt - 1, 21)
                    extra = max(0, D0 - c) if mt >= 1 else 0
                    pend.append((G + LAG + extra, seq, (lambda mt=mt, c=c: down(mt, c)))); seq += 1
                    if c == 21:
                        pend.append((G + LAG + extra, seq, (lambda mt=mt, cur=tiles_x[mt]: postnorm(mt, cur)))); seq += 1
                    ready = sorted([p_ for p_ in pend if p_[0] <= G], key=lambda p_: p_[1])
                    pend = [p_ for p_ in pend if p_[0] > G]
                    for p_ in ready:
                        p_[2]()
                    if c == 6 and mt + 1 < NT:
                        tiles_x[mt + 1] = [load_x1(2 * mt + 2), load_x1(2 * mt + 3)]
                    if c == 8 and mt + 1 < NT:
                        tiles_h[mt + 1] = h2_rot.get()
                        xas = [prenorm(x1_d, 2 * mt + 2 + s, tiles_x[mt + 1][s][0], tiles_x[mt + 1][s][1], xn_rot,
                                       tiles_h[mt + 1][0], tiles_h[mt + 1][1], s, A2, B2, split=True) for s in range(2)]
                    if c == 13 and mt + 1 < NT:
                        for s in range(2):
                            tb_ = 4 + up_i[0]
                            up_i[0] = (up_i[0] + 1) % 4
                            prenorm_b(xas[s][0], xas[s][1], tiles_h[mt + 1][0], tiles_h[mt + 1][1], s, A2, B2, tb=tb_)
                    G += 1
            gelu_prod(NT - 1, 21)
            for p_ in sorted(pend, key=lambda p_: p_[1]):
                p_[2]()
            S.finish("sp", b_out)
            S.barrier()
        S.emit()
    global LAST_SCHED
    LAST_SCHED = S
    return nc


def _prep_shared(inp):
    f = lambda a: np.ascontiguousarray(np.asarray(a, dtype=np.float32))
    col = lambda v: f(v).reshape(-1, 128).T
    colv = np.zeros((128, NCOLV), np.float32)
    colv[:, O_BADA:O_BADA + 48] = col(inp["b_ada"][0])
    colv[:, O_GPM:O_GPM + 8] = col(inp["g_pre_mix"][0])
    colv[:, O_GPF:O_GPF + 8] = col(inp["g_pre_ffn"][0])
    colv[:, O_BIN:O_BIN + 36] = col(inp["b_in"][0])
    wdw = f(inp["w_dw_conv"][0])
    for k in range(31):
        colv[:, O_WDW + k * 4:O_WDW + k * 4 + 4] = col(wdw[k])
    colv[:, O_BDW:O_BDW + 4] = col(inp["b_dw_conv"][0])
    colv[:, O_GLN:O_GLN + 4] = col(inp["g_conv_ln"][0])
    colv[:, O_BLN:O_BLN + 4] = col(inp["b_conv_ln"][0])
    colv[:, O_BCO:O_BCO + 8] = col(inp["b_conv_o"][0])
    wff = f(inp["w_dw_ffn"][0])
    for k in range(3):
        colv[:, O_WFF + k * 44:O_WFF + (k + 1) * 44] = col(wff[k])
    colv[:, O_BFF:O_BFF + 44] = col(inp["b_dw_ffn"][0])
    rowv = np.zeros((128, NROWV), np.float32)
    b_ada = f(inp["b_ada"][0])
    rowv[:, R_GPOSTM:R_GPOSTM + 1024] = f(inp["g_post_mix"][0])[None, :]
    rowv[:, R_GPOSTF:R_GPOSTF + 1024] = f(inp["g_post_ffn"][0])[None, :]
    rowv[:, R_BGTM:R_BGTM + 1024] = b_ada[2048:3072][None, :]
    rowv[:, R_BGTF:R_BGTF + 1024] = b_ada[5120:6144][None, :]
    rowv[:, R_BV:R_BV + 512] = f(inp["b_in"][0])[1024:1536][None, :]
    rb = f(inp["rel_bias"][0])
    key = np.arange(128)[:, None]
    q = np.arange(128)[None, :]
    biasT = np.zeros((128, 2, 8, 128), np.float32)
    for a, i in enumerate((3, 4)):
        rel = (4 - i) * 128 + q - key
        idx = np.clip(rel, -128, 128) + 128
        for h in range(8):
            biasT[:, a, h, :] = rb[h][idx]
    masked = (key >= 64) & (q < 64)
    biasT[:, 1, :, :][np.broadcast_to(masked[:, None, :], (128, 8, 128))] = NEG
    constb = np.broadcast_to(rb[:, 256][None, :], (128, 8)).astype(np.float32).copy()
    return dict(
        w_ada=f(inp["w_ada"][0]), w_in=f(inp["w_in"][0]), w_attn_o=f(inp["w_attn_o"][0]),
        w_conv_o=f(inp["w_conv_o"][0]), w_mix_o=f(inp["w_mix_o"][0]), w_up=f(inp["w_up"][0]),
        w_down=f(inp["w_down"][0]), colv=colv, rowv=rowv,
        biasT=np.ascontiguousarray(biasT.reshape(128, -1)), constb=constb)


_NC_CACHE = {}
LAST_SCHED = None


def kernel(**inputs):
    shared = _prep_shared(inputs)
    x = np.asarray(inputs["x"], dtype=np.float32)
    c = np.asarray(inputs["c"], dtype=np.float32)
    if "nc" not in _NC_CACHE:
        _NC_CACHE["nc"] = build_nc()
    nc = _NC_CACHE["nc"]
    in_maps = []
    for b in range(NCORES):
        m = dict(shared)
        m["x"] = np.ascontiguousarray(x[b])
        m["cT"] = np.ascontiguousarray(c[b].reshape(8, 128).T)
        in_maps.append(m)
    res = run_bass_kernel_spmd(nc, in_maps, core_ids=list(range(NCORES)))
    return np.stack([np.asarray(r["out"], dtype=np.float32) for r in res.results], axis=0)
```

```python
import numpy as np
from contextlib import ExitStack
import concourse.bass as bass
import concourse.mybir as mybir
from concourse.bass_utils import run_bass_kernel_spmd

F32 = mybir.dt.float32
BF16 = mybir.dt.bfloat16
AF = mybir.ActivationFunctionType
ALU = mybir.AluOpType

D = 1024
SEQ = 4096
NCORES = 8
DIN = 4608
DFF = 2816
EPS = 1e-6
T = 256
NT = SEQ // T
NEG = -30000.0

O_BADA, O_GPM, O_GPF, O_BIN, O_WDW, O_BDW, O_GLN, O_BLN, O_BCO, O_WFF, O_BFF = (
    0, 48, 56, 64, 100, 224, 228, 232, 236, 244, 376)
NCOLV = 420
R_GPOSTM, R_GPOSTF, R_BGTM, R_BGTF, R_BV = 0, 1024, 2048, 3072, 4096
NROWV = 4608


class Buf:
    __slots__ = ("name", "w", "r")

    def __init__(self, name=""):
        self.name = name
        self.w = None
        self.r = {}


class Sched:
    ENG = ("pe", "act", "dve", "pool", "sp")

    def __init__(self, nc, es, n_dma_sems=28):
        self.nc = nc
        self.prog = {e: [] for e in self.ENG}
        self.sem = {e: es.enter_context(nc.semaphore("s_" + e)) for e in self.ENG}
        self.cnt = {e: 0 for e in self.ENG}
        self.flag = {e: set() for e in self.ENG}
        self.dsem = [es.enter_context(nc.semaphore("d%d" % i)) for i in range(n_dma_sems)]
        self.dcnt = [0] * n_dma_sems
        self.dnx = {}
        self.known = {e: {} for e in self.ENG}

    def _need(self, e, key, val, waits):
        if key == e and e == "pe":
            return
        if self.known[e].get(key, 0) >= val:
            return
        if waits.get(key, 0) < val:
            waits[key] = val

    def _deps(self, e, reads, writes):
        waits = {}
        for b in reads:
            if b.w is not None:
                self._need(e, b.w[0], b.w[1], waits)
        for b in writes:
            if b.w is not None:
                self._need(e, b.w[0], b.w[1], waits)
            for k, v in b.r.items():
                self._need(e, k, v, waits)
        return waits

    def _emit_waits(self, e, waits):
        for key, val in waits.items():
            self.prog[e].append(("w", key, val))
            self.known[e][key] = val
            if isinstance(key, str):
                self.flag[key].add(val)

    def op(self, e, fn, reads=(), writes=()):
        self._emit_waits(e, self._deps(e, reads, writes))
        self.cnt[e] += 1
        v = self.cnt[e]
        self.prog[e].append(("o", fn, v))
        for b in reads:
            b.r[e] = v
        for b in writes:
            b.w = (e, v)
            b.r = {}
        return v

    def dma(self, e, fn, reads=(), writes=()):
        lo, hi = (0, 16) if e == "sp" else (16, len(self.dsem))
        i = self.dnx.get(e, lo)
        self.dnx[e] = lo + (i + 1 - lo) % (hi - lo)
        waits = self._deps(e, reads, writes)
        if self.dcnt[i] > 0:
            self._need(e, i, self.dcnt[i], waits)
        self._emit_waits(e, waits)
        self.dcnt[i] += 16
        v = self.dcnt[i]
        self.prog[e].append(("d", fn, i))
        for b in reads:
            b.r[i] = v
        for b in writes:
            b.w = (i, v)
            b.r = {}

    def barrier(self):
        for e in self.ENG:
            waits = {}
            for k in self.ENG:
                if k != e and self.cnt[k] > 0:
                    self._need(e, k, self.cnt[k], waits)
            for i, v in enumerate(self.dcnt):
                if v > 0:
                    self._need(e, i, v, waits)
            self._emit_waits(e, waits)

    def finish(self, e, bufs):
        waits = {}
        for b in bufs:
            if b.w is not None:
                self._need(e, b.w[0], b.w[1], waits)
        self._emit_waits(e, waits)

    def emit(self):
        nc = self.nc
        rank = {}
        for k in self.ENG:
            rank[k] = {v: i + 1 for i, v in enumerate(sorted(self.flag[k]))}

        def run(e, eng):
            for rec in self.prog[e]:
                if rec[0] == "w":
                    key, val = rec[1], rec[2]
                    if isinstance(key, str):
                        eng.wait_ge(self.sem[key], rank[key][val])
                    else:
                        eng.wait_ge(self.dsem[key], val)
                elif rec[0] == "o":
                    ins = rec[1](eng)
                    if rec[2] in rank[e]:
                        ins.then_inc(self.sem[e], 1)
                else:
                    rec[1](eng).then_inc(self.dsem[rec[2]], 16)

        with nc.Block() as block:
            @block.sync
            def _(eng):
                run("sp", eng)

            @block.tensor
            def _(eng):
                run("pe", eng)

            @block.scalar
            def _(eng):
                run("act", eng)

            @block.vector
            def _(eng):
                run("dve", eng)

            @block.gpsimd
            def _(eng):
                run("pool", eng)


class Rot:
    def __init__(self, tensors):
        self.items = [(t, Buf()) for t in tensors]
        self.i = 0

    def get(self):
        it = self.items[self.i]
        self.i = (self.i + 1) % len(self.items)
        return it


class StopBuild(Exception):
    pass


STOP = [None]
ATT_PIPE = True
import os
DENG = os.environ.get("DENG", "pool,pool,pool").split(",")


def build_nc(NT=NT):
    try:
        return _build_nc(NT)
    except StopBuild as ex:
        return ex.args[0]


def _build_nc(NT=NT):
    nc = bass.Bass("TRN2", target_bir_lowering=False)
    dt_in = lambda name, shape: nc.dram_tensor(name, shape, F32, kind="ExternalInput").ap()
    x_d = dt_in("x", [SEQ, D])
    cT_d = dt_in("cT", [128, 8])
    wada_d = dt_in("w_ada", [D, 6 * D])
    win_d = dt_in("w_in", [D, DIN])
    wao_d = dt_in("w_attn_o", [512, D])
    wco_d = dt_in("w_conv_o", [512, D])
    wmo_d = dt_in("w_mix_o", [D, D])
    wup_d = dt_in("w_up", [D, 2 * DFF])
    wdn_d = dt_in("w_down", [DFF, D])
    colv_d = dt_in("colv", [128, NCOLV])
    rowv_d = dt_in("rowv", [128, NROWV])
    biasT_d = dt_in("biasT", [128, 2 * 8 * 128])
    constb_d = dt_in("constb", [128, 8])
    out_d = nc.dram_tensor("out", [SEQ, D], F32, kind="ExternalOutput").ap()
    x1_d = nc.dram_tensor("x1s", [SEQ, D], F32, kind="Internal").ap()
    gf_d = nc.dram_tensor("gfs", [128, D], F32, kind="Internal").ap()
    bh_d = nc.dram_tensor("bhs", [128, 2048], BF16, kind="Internal").ap()
    bl_d = nc.dram_tensor("bls", [128, 2048], BF16, kind="Internal").ap()

    b_x1 = [Buf() for _ in range(SEQ // 128)]
    b_out = [Buf() for _ in range(SEQ // 128)]
    s_bufs = [Buf("S0"), Buf("S1")]

    with ExitStack() as es:
        S = Sched(nc, es)

        def ck(n):
            if STOP[0] == n:
                S.barrier()
                S.emit()
                global LAST_SCHED
                LAST_SCHED = S
                raise StopBuild(nc)
        sb = lambda st, name, shape, dt: st.enter_context(nc.sbuf_tensor("sb_" + name, shape, dt))

        colv = sb(es, "colv", [128, NCOLV], F32); b_colv = Buf()
        dv = sb(es, "dv", [128, 160], F32); b_dv = Buf()
        modc = sb(es, "modc", [128, 32], F32); b_modc = Buf()
        identb = sb(es, "identb", [128, 128], BF16); b_ident = Buf()
        onesf = sb(es, "onesf", [128, 128], F32); b_onesf = Buf()
        GP = sb(es, "GP", [128, D], F32); b_GP = Buf()
        b_GF = Buf(); b_gfd = Buf(); b_bhd = Buf(); b_bld = Buf()
        junk = sb(es, "junk", [128, D], BF16); b_junk = Buf()
        small = sb(es, "small", [128, 64], F32)
        small_rot = Rot([small[:, i:i + 1] for i in range(64)])
        psum = es.enter_context(nc.psum_tensor("ps_all", [128, 4096], F32))
        bk = [Buf("bank%d" % i) for i in range(8)]
        bank = lambda i: psum[:, i * 512:(i + 1) * 512]
        V_QB8, V_HGLB, V_HGAB, V_WDWH, V_GLNH, V_BLNH = 0, 4, 8, 24, 148, 152
        A1, B1, A2, B2 = 0, 8, 16, 24

        S.dma("sp", lambda e: e.dma_start(out=colv[:], in_=colv_d), writes=[b_colv])

        S.op("pool", lambda e: e.memset(onesf[:], 0.0), writes=[b_onesf])
        S.op("pool", lambda e: e.affine_select(out=onesf[:], in_=onesf[:], pattern=[[-1, 128]],
                                                compare_op=ALU.not_equal, fill=1.0, base=0,
                                                channel_multiplier=1),
             reads=[b_onesf], writes=[b_onesf])
        S.op("dve", lambda e: e.tensor_copy(out=identb[:], in_=onesf[:]), reads=[b_onesf], writes=[b_ident])
        S.op("pool", lambda e: e.memset(onesf[:], 1.0), reads=[b_onesf], writes=[b_onesf])

        def dcol(dst, n, src, mul):
            S.op("dve", lambda e: e.tensor_scalar(out=dv[:, dst:dst + n], in0=colv[:, src:src + n],
                                                  scalar1=mul, scalar2=None, op0=ALU.mult),
                 reads=[b_colv], writes=[b_dv])
        dcol(V_QB8, 4, O_BIN + 0, 0.125)
        dcol(V_HGLB, 4, O_BIN + 16, 0.5)
        dcol(V_HGAB, 16, O_BIN + 20, 0.5)
        dcol(V_WDWH, 124, O_WDW, 0.5)
        dcol(V_GLNH, 4, O_GLN, 0.5)
        dcol(V_BLNH, 4, O_BLN, 0.5)

        ck(1)
        with ExitStack() as s0:
            cT = sb(s0, "cT", [128, 8], F32); b_cT = Buf()
            cth = sb(s0, "cth", [128, 8], F32); b_cth = Buf()
            cact = sb(s0, "cact", [128, 8], F32); b_cact = Buf()
            cactb = sb(s0, "cactb", [128, 8], BF16); b_cactb = Buf()
            onesb = sb(s0, "onesb", [128, 128], BF16); b_onesb = Buf()
            crep = sb(s0, "crep", [128, 8, 128], BF16); b_crep = Buf()
            wa_rot = Rot([sb(s0, "wa%d" % i, [128, 8, 1024], BF16) for i in range(2)])
            rrow = Rot([sb(s0, "rr%d" % i, [128, 1024], F32) for i in range(2)])
            rtmp = sb(s0, "rtmp", [128, 1024], F32); b_rtmp = Buf()
            GFs = sb(s0, "GFs", [128, D], F32)

            S.dma("sp", lambda e: e.dma_start(out=cT[:], in_=cT_d), writes=[b_cT])
            S.op("act", lambda e: e.activation(out=cth[:], in_=cT[:], func=AF.Tanh, scale=0.5),
                 reads=[b_cT], writes=[b_cth])
            S.op("dve", lambda e: e.scalar_tensor_tensor(out=cact[:], in0=cth[:], scalar=1.0, in1=cT[:],
                                                         op0=ALU.add, op1=ALU.mult),
                 reads=[b_cth, b_cT], writes=[b_cact])
            S.op("dve", lambda e: e.tensor_scalar(out=cact[:], in0=cact[:], scalar1=0.5, scalar2=None,
                                                  op0=ALU.mult), reads=[b_cact], writes=[b_cact])
            S.op("dve", lambda e: e.tensor_copy(out=cactb[:], in_=cact[:]), reads=[b_cact], writes=[b_cactb])
            S.op("pool", lambda e: e.memset(onesb[:], 1.0), writes=[b_onesb])
            for dc in range(8):
                S.op("dve", lambda e, dc=dc: e.tensor_scalar(out=crep[:, dc, :], in0=onesb[:],
                                                             scalar1=cact[:, dc:dc + 1], scalar2=None,
                                                             op0=ALU.mult),
                     reads=[b_onesb, b_cact], writes=[b_crep])
            wada_v = wada_d.rearrange("(dc p) n -> p dc n", p=128)
            for g in range(6):
                wa, b_wa = wa_rot.get()
                S.dma("pool", lambda e, wa=wa, g=g: e.dma_start(out=wa[:], in_=wada_v[:, :, g * 1024:(g + 1) * 1024]),
                      writes=[b_wa])
                if g in (2, 5):
                    for half in range(2):
                        pb = bank(half)
                        for dc in range(8):
                            S.op("pe", lambda e, pb=pb, wa=wa, dc=dc, half=half: e.matmul(
                                pb, lhsT=crep[:, dc, :], rhs=wa[:, dc, half * 512:(half + 1) * 512],
                                start=(dc == 0), stop=(dc == 7)),
                                reads=[b_crep, b_wa], writes=[bk[half]])
                    r1, b_r1 = rrow.get()
                    r2, b_r2 = rrow.get()
                    o1 = R_BGTM if g == 2 else R_BGTF
                    o2 = R_GPOSTM if g == 2 else R_GPOSTF
                    S.dma("sp", lambda e, r1=r1, o1=o1: e.dma_start(out=r1[:], in_=rowv_d[:, o1:o1 + 1024]), writes=[b_r1])
                    S.dma("sp", lambda e, r2=r2, o2=o2: e.dma_start(out=r2[:], in_=rowv_d[:, o2:o2 + 1024]), writes=[b_r2])
                    G = GP if g == 2 else GFs
                    b_G = b_GP if g == 2 else b_GF
                    S.op("dve", lambda e, r1=r1: e.tensor_tensor(out=rtmp[:], in0=psum[:, 0:1024], in1=r1[:], op=ALU.add),
                         reads=[bk[0], bk[1], b_r1], writes=[b_rtmp])
                    S.op("dve", lambda e, r2=r2, G=G: e.tensor_tensor(out=G[:], in0=rtmp[:], in1=r2[:], op=ALU.mult),
                         reads=[b_rtmp, b_r2], writes=[b_G])
                else:
                    for oc in range(8):
                        col = 1024 + g * 8 + oc
                        for dc in range(8):
                            S.op("pe", lambda e, wa=wa, oc=oc, dc=dc, col=col: e.matmul(
                                psum[:, col:col + 1], lhsT=wa[:, dc, oc * 128:(oc + 1) * 128],
                                rhs=cactb[:, dc:dc + 1], start=(dc == 0), stop=(dc == 7)),
                                reads=[b_wa, b_cactb], writes=[bk[2]])
            mt_ = sb(s0, "modT", [128, 48], F32); b_mt = Buf()
            for c0 in (0, 24):
                S.op("dve", lambda e, c0=c0: e.tensor_tensor(out=mt_[:, c0:c0 + 16], in0=psum[:, 1024 + c0:1040 + c0],
                                                             in1=colv[:, O_BADA + c0:O_BADA + c0 + 16], op=ALU.add),
                     reads=[bk[2], b_colv], writes=[b_mt])
            S.op("dve", lambda e: e.scalar_tensor_tensor(out=modc[:, A1:A1 + 8], in0=mt_[:, 8:16], scalar=1.0,
                                                         in1=colv[:, O_GPM:O_GPM + 8], op0=ALU.add, op1=ALU.mult),
                 reads=[b_mt, b_colv], writes=[b_modc])
            S.op("dve", lambda e: e.tensor_copy(out=modc[:, B1:B1 + 8], in_=mt_[:, 0:8]), reads=[b_mt], writes=[b_modc])
            S.op("dve", lambda e: e.scalar_tensor_tensor(out=modc[:, A2:A2 + 8], in0=mt_[:, 32:40], scalar=1.0,
                                                         in1=colv[:, O_GPF:O_GPF + 8], op0=ALU.add, op1=ALU.mult),
                 reads=[b_mt, b_colv], writes=[b_modc])
            S.op("dve", lambda e: e.tensor_copy(out=modc[:, B2:B2 + 8], in_=mt_[:, 24:32]), reads=[b_mt], writes=[b_modc])
            S.dma("sp", lambda e: e.dma_start(out=gf_d, in_=GFs[:]), reads=[b_GF], writes=[b_gfd])
            bt32 = sb(s0, "bt32", [128, 2048], F32); b_bt32 = Buf()
            bhi = sb(s0, "bhi", [128, 2048], BF16); b_bhi = Buf()
            bhf = sb(s0, "bhf", [128, 2048], F32); b_bhf = Buf()
            blo = sb(s0, "blo", [128, 2048], BF16); b_blo = Buf()
            S.dma("sp", lambda e: e.dma_start(out=bt32[:], in_=biasT_d), writes=[b_bt32])
            S.op("dve", lambda e: e.tensor_copy(out=bhi[:], in_=bt32[:]), reads=[b_bt32], writes=[b_bhi])
            S.op("dve", lambda e: e.tensor_copy(out=bhf[:], in_=bhi[:]), reads=[b_bhi], writes=[b_bhf])
            S.op("dve", lambda e: e.tensor_tensor(out=blo[:], in0=bt32[:], in1=bhf[:], op=ALU.subtract),
                 reads=[b_bt32, b_bhf], writes=[b_blo])
            S.dma("sp", lambda e: e.dma_start(out=bh_d, in_=bhi[:]), reads=[b_bhi], writes=[b_bhd])
            S.dma("sp", lambda e: e.dma_start(out=bl_d, in_=blo[:]), reads=[b_blo], writes=[b_bld])
            S.barrier()

        ck(2)
        tp_bufs = [Buf("tp0"), Buf("tp1")]
        tpv = psum[:, 7 * 512:8 * 512].bitcast(BF16)
        tp_rot_i = [0]

        def prenorm(src_d, j, xin, b_xin, xn_rot, hT, b_hT, s, Acol, Bcol, split=False):
            ss, b_ss = small_rot.get()
            S.op("act", lambda e: e.activation(out=junk[:], in_=xin[:], func=AF.Square, accum_out=ss),
                 reads=[b_xin], writes=[b_junk, b_ss])
            r1, b_r1 = small_rot.get()
            S.op("dve", lambda e: e.tensor_scalar(out=r1, in0=ss, scalar1=1.0 / D, scalar2=EPS,
                                                  op0=ALU.mult, op1=ALU.add), reads=[b_ss], writes=[b_r1])
            r2, b_r2 = small_rot.get()
            S.op("act", lambda e: e.activation(out=r2, in_=r1, func=AF.Sqrt), reads=[b_r1], writes=[b_r2])
            S.op("dve", lambda e: e.reciprocal(out=r2, in_=r2), reads=[b_r2], writes=[b_r2])
            xn, b_xn = xn_rot.get()
            S.op("dve", lambda e: e.tensor_scalar(out=xn[:], in0=xin[:], scalar1=r2, scalar2=None, op0=ALU.mult),
                 reads=[b_xin, b_r2], writes=[b_xn])
            if split:
                return xn, b_xn
            prenorm_b(xn, b_xn, hT, b_hT, s, Acol, Bcol)

        def prenorm_b(xn, b_xn, hT, b_hT, s, Acol, Bcol, tb=7):
            tpv = psum[:, tb * 512:(tb + 1) * 512].bitcast(BF16)
            for dc in range(8):
                S.op("pe", lambda e, dc=dc, xn=xn: e.transpose(
                    tpv[:, dc * 128:(dc + 1) * 128], xn[:, dc * 128:(dc + 1) * 128], identb[:]),
                    reads=[b_xn, b_ident], writes=[bk[tb]])
            for dc in range(8):
                if dc % 2 == 0:
                    S.op("act", lambda e, dc=dc: e.activation(
                        out=hT[:, dc, s * 128:(s + 1) * 128], in_=tpv[:, dc * 128:(dc + 1) * 128],
                        func=AF.Identity, scale=modc[:, Acol + dc:Acol + dc + 1],
                        bias=modc[:, Bcol + dc:Bcol + dc + 1]),
                        reads=[bk[tb], b_modc], writes=[b_hT])
                else:
                    S.op("dve", lambda e, dc=dc: e.tensor_scalar(
                        out=hT[:, dc, s * 128:(s + 1) * 128], in0=tpv[:, dc * 128:(dc + 1) * 128],
                        scalar1=modc[:, Acol + dc:Acol + dc + 1], scalar2=modc[:, Bcol + dc:Bcol + dc + 1],
                        op0=ALU.mult, op1=ALU.add),
                        reads=[bk[tb], b_modc], writes=[b_hT])

        with ExitStack() as p1:
            win = sb(p1, "win", [128, 8, DIN], BF16)
            b_win = [Buf() for _ in range(4)]
            wao = sb(p1, "wao", [128, 4, D], BF16); b_wao = Buf()
            wco = sb(p1, "wco", [128, 4, D], BF16); b_wco = Buf()
            wmo = sb(p1, "wmo", [128, 8, D], BF16); b_wmo = Buf()
            Bh = sb(p1, "Bh", [128, 2, 8, 128], BF16); b_Bh = Buf()
            Bl = sb(p1, "Bl", [128, 2, 8, 128], BF16); b_Bl = Buf()
            Lm = sb(p1, "Lm", [1, 128], BF16); Rm = sb(p1, "Rm", [1, 128], BF16); b_LR = Buf()
            S.op("pool", lambda e: e.memset(Lm[:, 0:64], 1.0), writes=[b_LR])
            S.op("pool", lambda e: e.memset(Lm[:, 64:128], 0.0), writes=[b_LR])
            S.op("pool", lambda e: e.memset(Rm[:, 0:64], 0.0), writes=[b_LR])
            S.op("pool", lambda e: e.memset(Rm[:, 64:128], NEG), writes=[b_LR])
            constb = sb(p1, "constb", [128, 8], F32); b_cb = Buf()
            bvb = sb(p1, "bvb", [128, 512], F32); b_bvb = Buf()
            xin_rot = Rot([sb(p1, "xin%d" % i, [128, D], F32) for i in range(2)])
            xres_rot = Rot([sb(p1, "xres%d" % i, [128, D], F32) for i in range(1)])
            xn_rot = Rot([sb(p1, "xn%d" % i, [128, D], BF16) for i in range(1)])
            hT2 = [(sb(p1, "hT%d" % i, [128, 8, T], BF16), Buf()) for i in range(2)]
            QT2 = [(sb(p1, "QT%d" % i, [128, 4, T], BF16), Buf()) for i in range(2)]
            KT = sb(p1, "KT", [128, 4, 1024], BF16)
            b_KT = [Buf() for _ in range(4)]
            Vr = sb(p1, "Vr", [128, 8, 8, 65], BF16)
            b_Vr = [Buf() for _ in range(8)]
            u2 = [(sb(p1, "u%d" % i, [128, 4, 30 + T], BF16), [Buf() for _ in range(4)]) for i in range(2)]
            cacc = sb(p1, "cacc", [128, 4, T], F32)
            b_cacc = [Buf() for _ in range(4)]
            stA = sb(p1, "stA", [128, T], F32); b_stA = Buf()
            stB = sb(p1, "stB", [128, T], F32); b_stB = Buf()
            uT2 = [(sb(p1, "uT%d" % i, [128, 4, T], BF16), Buf()) for i in range(2)]
            f_rot = Rot([sb(p1, "f%d" % i, [128, T], F32) for i in range(9)])
            PT_rot = Rot([sb(p1, "PT%d" % i, [128, 5, 128], BF16) for i in range(2)])
            rc_t = sb(p1, "rc", [128, 16], F32)
            rc_rot = Rot([rc_t[:, i * 4:(i + 1) * 4] for i in range(4)])
            attn_rot = Rot([sb(p1, "attn%d" % i, [128, 512], BF16) for i in range(1)])
            attnT2 = [(sb(p1, "attnT%d" % i, [128, 4, T], BF16), Buf()) for i in range(2)]
            tg_rot = Rot([sb(p1, "tg%d" % i, [128, 2, T], F32) for i in range(1)])
            yT2 = [(sb(p1, "yT%d" % i, [128, 8, T], BF16), Buf()) for i in range(2)]

            win_v = win_d.rearrange("(dc p) n -> p dc n", p=128)
            groups = [(0, 1536), (1536, 2560), (2560, 3584), (3584, 4608)]
            for gi, (c0, c1) in enumerate(groups):
                S.dma("pool", lambda e, c0=c0, c1=c1: e.dma_start(out=win[:, :, c0:c1], in_=win_v[:, :, c0:c1]),
                      writes=[b_win[gi]])
            S.dma("sp", lambda e: e.dma_start(out=Bh[:].rearrange("p a h q -> p (a h q)"), in_=bh_d), reads=[b_bhd], writes=[b_Bh])
            S.dma("sp", lambda e: e.dma_start(out=Bl[:].rearrange("p a h q -> p (a h q)"), in_=bl_d), reads=[b_bld], writes=[b_Bl])
            S.dma("sp", lambda e: e.dma_start(out=constb[:], in_=constb_d), writes=[b_cb])
            S.dma("sp", lambda e: e.dma_start(out=bvb[:], in_=rowv_d[:, R_BV:R_BV + 512]), writes=[b_bvb])
            S.dma("pool", lambda e: e.dma_start(out=wao[:], in_=wao_d.rearrange("(c p) n -> p c n", p=128)), writes=[b_wao])
            S.dma("pool", lambda e: e.dma_start(out=wco[:], in_=wco_d.rearrange("(c p) n -> p c n", p=128)), writes=[b_wco])
            S.dma("pool", lambda e: e.dma_start(out=wmo[:], in_=wmo_d.rearrange("(c p) n -> p c n", p=128)), writes=[b_wmo])
            S.op("pool", lambda e: e.memset(Vr[:].rearrange("p a h d -> p (a h d)"), 1.0), writes=b_Vr)
            S.op("pool", lambda e: e.memset(u2[0][0][:, :, 0:30], 0.0), writes=u2[0][1])

            ck(3)
            mm_i = [0]

            MMB = [0, 1, 7]

            def mmidx():
                i = MMB[mm_i[0]]
                mm_i[0] = (mm_i[0] + 1) % len(MMB)
                return i

            def mmbank():
                i = mmidx()
                return bank(i), bk[i]

            def load_x(j):
                xin, b_xin = xin_rot.get()
                S.dma("sp", lambda e, xin=xin, j=j: e.dma_start(out=xin[:], in_=x_d[j * 128:(j + 1) * 128, :]),
                      writes=[b_xin])
                return xin, b_xin

            def tile_gen(mt):
                par = mt % 2
                hT, b_hT = hT2[par]
                QT, b_QT = QT2[par]
                u, b_u = u2[par]
                u_o, b_uo = u2[1 - par]
                uT, b_uT = uT2[par]
                attnT, b_attnT = attnT2[par]
                yT, b_yT = yT2[par]
                cur = [load_x(2 * mt), load_x(2 * mt + 1)]
                ring = mt % 4
                for s in range(2):
                    xa = prenorm(x_d, 2 * mt + s, cur[s][0], cur[s][1], xn_rot, hT, b_hT, s, A1, B1, split=True)
                    yield 0
                    prenorm_b(xa[0], xa[1], hT, b_hT, s, A1, B1, tb=mmidx())
                    yield 0
                def qk_group(oc):
                    pb, bb = mmbank()
                    for dc in range(8):
                        S.op("pe", lambda e, pb=pb, oc=oc, dc=dc: e.matmul(
                            pb[:, 0:T], lhsT=win[:, dc, oc * 128:(oc + 1) * 128], rhs=hT[:, dc, :],
                            start=(dc == 0), stop=(dc == 7)), reads=[b_win[0], b_hT], writes=[bb])
                    if oc < 4:
                        S.op("act", lambda e, pb=pb, oc=oc: e.activation(
                            out=QT[:, oc, :], in_=pb[:, 0:T], func=AF.Identity, scale=0.125,
                            bias=dv[:, V_QB8 + oc:V_QB8 + oc + 1]), reads=[bb, b_dv], writes=[b_QT])
                    else:
                        S.op("act", lambda e, pb=pb, oc=oc, ring=ring: e.activation(
                            out=KT[:, oc - 4, ring * T:(ring + 1) * T], in_=pb[:, 0:T], func=AF.Identity,
                            bias=colv[:, O_BIN + oc:O_BIN + oc + 1]),
                            reads=[bb, b_colv], writes=[b_KT[ring]])

                def v_group(s):
                    j = 2 * mt + s
                    pb, bb = mmbank()
                    for dc in range(8):
                        S.op("pe", lambda e, pb=pb, dc=dc, s=s: e.matmul(
                            pb, lhsT=hT[:, dc, s * 128:(s + 1) * 128], rhs=win[:, dc, 1024:1536],
                            start=(dc == 0), stop=(dc == 7)), reads=[b_win[0], b_hT], writes=[bb])
                    S.op("dve", lambda e, pb=pb, j=j: e.tensor_tensor(
                        out=Vr[:, j % 8, :, 0:64], in0=pb.rearrange("p (h d) -> p h d", h=8),
                        in1=bvb[:].rearrange("p (h d) -> p h d", h=8), op=ALU.add),
                        reads=[bb, b_bvb], writes=[b_Vr[j % 8]])

                deferred = [(lambda oc=oc: qk_group(oc)) for oc in range(8)] + [(lambda s=s: v_group(s)) for s in range(2)]
                for cc in range(4):
                    pb, bb = mmbank()
                    for half, base in ((0, 1536), (1, 2048)):
                        for dc in range(8):
                            S.op("pe", lambda e, pb=pb, cc=cc, dc=dc, half=half, base=base: e.matmul(
                                pb[:, half * T:(half + 1) * T],
                                lhsT=win[:, dc, base + cc * 128:base + (cc + 1) * 128], rhs=hT[:, dc, :],
                                start=(dc == 0), stop=(dc == 7)), reads=[b_win[1], b_hT], writes=[bb])
                    tgl, b_tgl = f_rot.get()
                    S.op("act", lambda e, pb=pb, cc=cc, tgl=tgl: e.activation(
                        out=tgl[:], in_=pb[:, T:2 * T], func=AF.Tanh, scale=0.5,
                        bias=dv[:, V_HGLB + cc:V_HGLB + cc + 1]), reads=[bb, b_dv], writes=[b_tgl])
                    ab, b_ab = f_rot.get()
                    S.op("act", lambda e, pb=pb, cc=cc, ab=ab: e.activation(
                        out=ab[:], in_=pb[:, 0:T], func=AF.Identity,
                        bias=colv[:, O_BIN + 12 + cc:O_BIN + 13 + cc]), reads=[bb, b_colv], writes=[b_ab])
                    S.op("dve", lambda e, cc=cc, tgl=tgl, ab=ab: e.scalar_tensor_tensor(
                        out=u[:, cc, 30:30 + T], in0=tgl[:], scalar=1.0, in1=ab[:], op0=ALU.add, op1=ALU.mult),
                        reads=[b_tgl, b_ab], writes=[b_u[cc]])
                    yield 0
                yield 1
                for k in range(31):
                    for cc in range(4):
                        eng = "dve"
                        wcol = dv[:, V_WDWH + k * 4 + cc:V_WDWH + k * 4 + cc + 1]
                        if k == 0:
                            S.op(eng, lambda e, cc=cc, wcol=wcol: e.tensor_scalar(
                                out=cacc[:, cc, :], in0=u[:, cc, 0:T], scalar1=wcol,
                                scalar2=colv[:, O_BDW + cc:O_BDW + cc + 1], op0=ALU.mult, op1=ALU.add),
                                reads=[b_u[cc], b_dv, b_colv], writes=[b_cacc[cc]])
                        else:
                            S.op(eng, lambda e, cc=cc, k=k, wcol=wcol: e.scalar_tensor_tensor(
                                out=cacc[:, cc, :], in0=u[:, cc, k:k + T], scalar=wcol, in1=cacc[:, cc, :],
                                op0=ALU.mult, op1=ALU.add),
                                reads=[b_u[cc], b_dv, b_cacc[cc]], writes=[b_cacc[cc]])
                    if k % 3 == 1 and deferred:
                        deferred.pop(0)()
                    yield 1
                while deferred:
                    deferred.pop(0)()
                for cc in range(4):
                    eng = "dve" if cc < 2 else "pool"
                    S.op(eng, lambda e, cc=cc: e.tensor_copy(out=u_o[:, cc, 0:30], in_=u[:, cc, T:T + 30]),
                         reads=[b_u[cc]], writes=[b_uo[cc]])
                pb, bb = mmbank()
                for cc in range(4):
                    S.op("pe", lambda e, pb=pb, cc=cc: e.matmul(pb[:, 0:T], lhsT=onesf[:], rhs=cacc[:, cc, :],
                                                                 start=(cc == 0), stop=(cc == 3)),
                         reads=[b_onesf, b_cacc[cc]], writes=[bb])
                for cc in range(4):
                    sq, b_sq = f_rot.get()
                    S.op("act", lambda e, cc=cc, sq=sq: e.activation(out=sq[:], in_=cacc[:, cc, :], func=AF.Square),
                         reads=[b_cacc[cc]], writes=[b_sq])
                    S.op("pe", lambda e, pb=pb, cc=cc, sq=sq: e.matmul(pb[:, T:2 * T], lhsT=onesf[:], rhs=sq[:],
                                                                        start=(cc == 0), stop=(cc == 3)),
                         reads=[b_onesf, b_sq], writes=[bb])
                mean, b_mean = f_rot.get()
                S.op("dve", lambda e, pb=pb, mean=mean: e.tensor_scalar(out=mean[:], in0=pb[:, 0:T], scalar1=1.0 / 512,
                                                                        scalar2=None, op0=ALU.mult),
                     reads=[bb], writes=[b_mean])
                var, b_var = f_rot.get()
                S.op("dve", lambda e, mean=mean, var=var: e.tensor_tensor(out=var[:], in0=mean[:], in1=mean[:], op=ALU.mult),
                     reads=[b_mean], writes=[b_var])
                S.op("dve", lambda e, pb=pb, var=var: e.scalar_tensor_tensor(
                    out=var[:], in0=pb[:, T:2 * T], scalar=1.0 / 512, in1=var[:], op0=ALU.mult, op1=ALU.subtract),
                    reads=[bb, b_var], writes=[b_var])
                S.op("dve", lambda e, var=var: e.tensor_scalar(out=var[:], in0=var[:], scalar1=EPS, scalar2=None,
                                                                op0=ALU.add), reads=[b_var], writes=[b_var])
                S.op("act", lambda e, var=var: e.activation(out=stA[:], in_=var[:], func=AF.Sqrt),
                     reads=[b_var], writes=[b_stA])
                S.op("dve", lambda e: e.reciprocal(out=stA[:], in_=stA[:]), reads=[b_stA], writes=[b_stA])
                S.op("dve", lambda e, mean=mean: e.scalar_tensor_tensor(
                    out=stB[:], in0=mean[:], scalar=-1.0, in1=stA[:], op0=ALU.mult, op1=ALU.mult),
                    reads=[b_mean, b_stA], writes=[b_stB])
                yield 1
                for cc in range(4):
                    eng = "dve" if cc < 2 else "pool"
                    S.op(eng, lambda e, cc=cc: e.tensor_tensor(out=cacc[:, cc, :], in0=cacc[:, cc, :], in1=stA[:], op=ALU.mult),
                         reads=[b_cacc[cc], b_stA], writes=[b_cacc[cc]])
                    S.op(eng, lambda e, cc=cc: e.tensor_tensor(out=cacc[:, cc, :], in0=cacc[:, cc, :], in1=stB[:], op=ALU.add),
                         reads=[b_cacc[cc], b_stB], writes=[b_cacc[cc]])
                    yh, b_yh = f_rot.get()
                    S.op("act", lambda e, cc=cc, yh=yh: e.activation(
                        out=yh[:], in_=cacc[:, cc, :], func=AF.Identity, scale=dv[:, V_GLNH + cc:V_GLNH + cc + 1],
                        bias=dv[:, V_BLNH + cc:V_BLNH + cc + 1]), reads=[b_cacc[cc], b_dv], writes=[b_yh])
                    th, b_th = f_rot.get()
                    S.op("act", lambda e, yh=yh, th=th: e.activation(out=th[:], in_=yh[:], func=AF.Tanh),
                         reads=[b_yh], writes=[b_th])
                    S.op("dve", lambda e, cc=cc, yh=yh, th=th: e.scalar_tensor_tensor(
                        out=uT[:, cc, :], in0=th[:], scalar=1.0, in1=yh[:], op0=ALU.add, op1=ALU.mult),
                        reads=[b_th, b_yh], writes=[b_uT])
                    yield 1
                yield 2
                for s in range(2):
                    j = 2 * mt + s
                    i0 = max(0, 4 - j)
                    at, b_at = attn_rot.get()
                    pv = psum[:, 6 * 512:6 * 512 + 260].rearrange("p (h d) -> p h d", d=65)

                    def emit_S(h, s=s, j=j, i0=i0):
                        hp, po = h // 2, (h % 2) * 64
                        so = 1024 + (h % 2) * 1024
                        for i in range(i0, 5):
                            kb = j - 4 + i
                            rs, ro = (kb // 2) % 4, (kb % 2) * 128
                            S.op("pe", lambda e, so=so, i=i, po=po, hp=hp, rs=rs, ro=ro, s=s: e.matmul(
                                psum[:, so + i * 128:so + (i + 1) * 128],
                                lhsT=KT[po:po + 64, hp, rs * T + ro:rs * T + ro + 128],
                                rhs=QT[po:po + 64, hp, s * 128:(s + 1) * 128], start=True, stop=(0 < i < 3 or (i == 0 and i0 > 0))),
                                reads=[b_KT[rs], b_QT], writes=[s_bufs[h % 2]])
                            if i == 0 and i0 == 0:
                                S.op("pe", lambda e, so=so: e.matmul(
                                    psum[:, so:so + 128], lhsT=Lm[:], rhs=Rm[:], start=False, stop=True),
                                    reads=[b_LR], writes=[s_bufs[h % 2]])
                            if i >= 3:
                                S.op("pe", lambda e, so=so, i=i, h=h: e.matmul(
                                    psum[:, so + i * 128:so + (i + 1) * 128], lhsT=identb[:], rhs=Bh[:, i - 3, h, :],
                                    start=False, stop=False), reads=[b_ident, b_Bh], writes=[s_bufs[h % 2]])
                                S.op("pe", lambda e, so=so, i=i, h=h: e.matmul(
                                    psum[:, so + i * 128:so + (i + 1) * 128], lhsT=identb[:], rhs=Bl[:, i - 3, h, :],
                                    start=False, stop=True), reads=[b_ident, b_Bl], writes=[s_bufs[h % 2]])

                    def emit_soft(h, i0=i0):
                        so = 1024 + (h % 2) * 1024
                        bS = s_bufs[h % 2]
                        PT, b_PT = PT_rot.get()
                        if i0 < 3:
                            S.op("act", lambda e, so=so, i0=i0, PT=PT, h=h: e.activation(
                                out=PT[:, i0:3, :],
                                in_=psum[:, so + i0 * 128:so + 3 * 128].rearrange("p (a q) -> p a q", q=128),
                                func=AF.Exp, bias=constb[:, h:h + 1]), reads=[bS, b_cb], writes=[b_PT])
                        t0 = max(i0, 3)
                        S.op("act", lambda e, so=so, t0=t0, PT=PT: e.activation(
                            out=PT[:, t0:5, :],
                            in_=psum[:, so + t0 * 128:so + 5 * 128].rearrange("p (a q) -> p a q", q=128),
                            func=AF.Exp), reads=[bS, b_PT], writes=[b_PT])
                        return PT, b_PT

                    def emit_PV(h, PT, b_PT, j=j, i0=i0, pv=pv):
                        for i in range(i0, 5):
                            kb = j - 4 + i
                            S.op("pe", lambda e, pv=pv, PT=PT, i=i, kb=kb, h=h, i0=i0: e.matmul(
                                pv[:, h % 4, :], lhsT=PT[:, i, :], rhs=Vr[:, kb % 8, h, :],
                                start=(i == i0), stop=(i == 4)),
                                reads=[b_PT, b_Vr[kb % 8]], writes=[bk[6]])

                    if ATT_PIPE:
                        emit_S(0)
                    for h in range(8):
                        if not ATT_PIPE:
                            emit_S(h)
                        PT, b_PT = emit_soft(h)
                        if ATT_PIPE and h + 1 < 8:
                            emit_S(h + 1)
                        emit_PV(h, PT, b_PT)
                        if h % 4 == 3:
                            half = h // 4
                            rc, b_rc = rc_rot.get()
                            S.op("dve", lambda e, pv=pv, rc=rc: e.reciprocal(out=rc, in_=pv[:, :, 64]),
                                 reads=[bk[6]], writes=[b_rc])
                            S.op("dve", lambda e, pv=pv, rc=rc, at=at, half=half: e.tensor_tensor(
                                out=at[:, half * 256:(half + 1) * 256].rearrange("p (h d) -> p h d", d=64),
                                in0=pv[:, :, 0:64], in1=rc.unsqueeze(2).to_broadcast([128, 4, 64]), op=ALU.mult),
                                reads=[bk[6], b_rc, b_at], writes=[b_at])
                        yield 2
                    tb_ = mmidx()
                    tpa = psum[:, tb_ * 512:(tb_ + 1) * 512].bitcast(BF16)
                    for c4 in range(4):
                        S.op("pe", lambda e, c4=c4, at=at, tpa=tpa: e.transpose(
                            tpa[:, c4 * 128:(c4 + 1) * 128], at[:, c4 * 128:(c4 + 1) * 128], identb[:]),
                            reads=[b_at, b_ident], writes=[bk[tb_]])
                    S.op("act", lambda e, s=s, tpa=tpa: e.activation(
                        out=attnT[:, :, s * 128:(s + 1) * 128], in_=tpa[:, 0:512].rearrange("p (c q) -> p c q", q=128),
                        func=AF.Identity), reads=[bk[tb_]], writes=[b_attnT])
                    yield 2
                yield 3
                for fc in range(8):
                    pg, bg = mmbank()
                    for half, base, gi in ((0, 2560, 2), (1, 3584, 3)):
                        for dc in range(8):
                            S.op("pe", lambda e, pg=pg, fc=fc, dc=dc, half=half, base=base: e.matmul(
                                pg[:, half * T:(half + 1) * T],
                                lhsT=win[:, dc, base + fc * 128:base + (fc + 1) * 128], rhs=hT[:, dc, :],
                                start=(dc == 0), stop=(dc == 7)), reads=[b_win[gi], b_hT], writes=[bg])
                    pa, ba = mmbank()
                    for c4 in range(4):
                        S.op("pe", lambda e, pa=pa, fc=fc, c4=c4: e.matmul(
                            pa[:, 0:T], lhsT=wao[:, c4, fc * 128:(fc + 1) * 128], rhs=attnT[:, c4, :],
                            start=(c4 == 0), stop=(c4 == 3)), reads=[b_wao, b_attnT], writes=[ba])
                    for c4 in range(4):
                        S.op("pe", lambda e, pa=pa, fc=fc, c4=c4: e.matmul(
                            pa[:, T:2 * T], lhsT=wco[:, c4, fc * 128:(fc + 1) * 128], rhs=uT[:, c4, :],
                            start=(c4 == 0), stop=(c4 == 3)), reads=[b_wco, b_uT], writes=[ba])
                    tg, b_tg = tg_rot.get()
                    for half in range(2):
                        S.op("act", lambda e, pg=pg, tg=tg, half=half, fc=fc: e.activation(
                            out=tg[:, half, :], in_=pg[:, half * T:(half + 1) * T], func=AF.Tanh, scale=0.5,
                            bias=dv[:, V_HGAB + half * 8 + fc:V_HGAB + half * 8 + fc + 1]),
                            reads=[bg, b_dv], writes=[b_tg])
                    asb, b_asb = f_rot.get()
                    S.op("act", lambda e, pa=pa, asb=asb: e.activation(out=asb[:], in_=pa[:, 0:T], func=AF.Identity),
                         reads=[ba], writes=[b_asb])
                    cbb, b_cbb = f_rot.get()
                    S.op("act", lambda e, pa=pa, cbb=cbb, fc=fc: e.activation(
                        out=cbb[:], in_=pa[:, T:2 * T], func=AF.Identity,
                        bias=colv[:, O_BCO + fc:O_BCO + fc + 1]), reads=[ba, b_colv], writes=[b_cbb])
                    t1, b_t1 = f_rot.get()
                    S.op("dve", lambda e, tg=tg, asb=asb, t1=t1: e.scalar_tensor_tensor(
                        out=t1[:], in0=tg[:, 0, :], scalar=1.0, in1=asb[:], op0=ALU.add, op1=ALU.mult),
                        reads=[b_tg, b_asb], writes=[b_t1])
                    t2, b_t2 = f_rot.get()
                    S.op("dve", lambda e, tg=tg, cbb=cbb, t2=t2: e.scalar_tensor_tensor(
                        out=t2[:], in0=tg[:, 1, :], scalar=1.0, in1=cbb[:], op0=ALU.add, op1=ALU.mult),
                        reads=[b_tg, b_cbb], writes=[b_t2])
                    S.op("dve", lambda e, t1=t1, t2=t2, fc=fc: e.tensor_tensor(
                        out=yT[:, fc, :], in0=t1[:], in1=t2[:], op=ALU.add),
                        reads=[b_t1, b_t2], writes=[b_yT])
                    yield 3
                yield 4
                for s in range(2):
                    j = 2 * mt + s
                    xres, b_xres = xres_rot.get()
                    S.dma("sp", lambda e, xres=xres, j=j: e.dma_start(out=xres[:], in_=x_d[j * 128:(j + 1) * 128, :]),
                          writes=[b_xres])
                    pm = psum[:, 2 * 512:4 * 512]
                    for half in range(2):
                        for dc in range(8):
                            S.op("pe", lambda e, pm=pm, half=half, dc=dc, s=s: e.matmul(
                                pm[:, half * 512:(half + 1) * 512], lhsT=yT[:, dc, s * 128:(s + 1) * 128],
                                rhs=wmo[:, dc, half * 512:(half + 1) * 512], start=(dc == 0), stop=(dc == 7)),
                                reads=[b_yT, b_wmo], writes=[s_bufs[0]])
                    ss, b_ss = small_rot.get()
                    S.op("act", lambda e, pm=pm, ss=ss: e.activation(out=junk[:], in_=pm, func=AF.Square, accum_out=ss),
                         reads=[s_bufs[0]], writes=[b_junk, b_ss])
                    r1, b_r1 = small_rot.get()
                    S.op("dve", lambda e, ss=ss, r1=r1: e.tensor_scalar(out=r1, in0=ss, scalar1=1.0 / D, scalar2=4 * EPS,
                                                                        op0=ALU.mult, op1=ALU.add),
                         reads=[b_ss], writes=[b_r1])
                    r2, b_r2 = small_rot.get()
                    S.op("act", lambda e, r1=r1, r2=r2: e.activation(out=r2, in_=r1, func=AF.Sqrt), reads=[b_r1], writes=[b_r2])
                    S.op("dve", lambda e, r2=r2: e.reciprocal(out=r2, in_=r2), reads=[b_r2], writes=[b_r2])
                    S.op("dve", lambda e, pm=pm, r2=r2: e.scalar_tensor_tensor(
                        out=pm, in0=pm, scalar=r2, in1=GP[:], op0=ALU.mult, op1=ALU.mult),
                        reads=[s_bufs[0], b_r2, b_GP], writes=[s_bufs[0]])
                    S.op("dve", lambda e, pm=pm, xres=xres: e.tensor_tensor(out=xres[:], in0=pm, in1=xres[:], op=ALU.add),
                         reads=[s_bufs[0], b_xres], writes=[b_xres])
                    S.dma("sp", lambda e, xres=xres, j=j: e.dma_start(out=x1_d[j * 128:(j + 1) * 128, :], in_=xres[:]),
                          reads=[b_xres], writes=[b_x1[j]])
                    yield 4

            active = []
            st = {}
            next_mt = 0
            while active or next_mt < NT:
                can_start = next_mt < NT and (
                    not active
                    or (len(active) == 1 and st[active[0]] >= 1)
                    or (len(active) == 2 and st[active[0]] >= 4 and st[active[1]] >= 1))
                if can_start:
                    g = tile_gen(next_mt)
                    next_mt += 1
                    active.append(g)
                    st[g] = 0
                for g in list(active):
                    k = active.index(g)
                    if k == 0 or st[g] < st[active[k - 1]]:
                        try:
                            st[g] = next(g)
                        except StopIteration:
                            active.remove(g)
                            del st[g]
            ck(9)
            S.barrier()

        with ExitStack() as p2:
            wup = sb(p2, "wup", [128, 8, 2 * DFF], BF16)
            b_wupq = [[Buf() for _ in range(8)] for _ in range(4)]
            wdn = sb(p2, "wdn", [128, 22, D], BF16)
            b_wdn = [Buf() for _ in range(22)]
            xin_rot = Rot([sb(p2, "x2in%d" % i, [128, D], F32) for i in range(4)])
            xn_rot = Rot([sb(p2, "x2n%d" % i, [128, D], BF16) for i in range(2)])
            h2_rot = Rot([sb(p2, "h2T%d" % i, [128, 8, T], BF16) for i in range(2)])
            us_rot = Rot([sb(p2, "us%d" % i, [128, T + 2], F32) for i in range(8)])
            halo = sb(p2, "halo", [128, 44, 2], F32); b_halo = [Buf() for _ in range(44)]
            f_rot = Rot([sb(p2, "g%d" % i, [128, T], F32) for i in range(15)])
            pr_rot = Rot([sb(p2, "pr%d" % i, [128, T], BF16) for i in range(13)])
            GF = sb(p2, "GF", [128, D], F32); b_GF = Buf()
            S.dma("sp", lambda e: e.dma_start(out=GF[:], in_=gf_d), reads=[b_gfd], writes=[b_GF])

            QCOL = [(0, 1408), (2816, 4224), (1408, 2816), (4224, 5632)]
            for qs, crange in (((0, 1), range(0, 11)), ((2, 3), range(11, 22))):
                for q in qs:
                    c0, c1 = QCOL[q]
                    for dc in range(8):
                        S.dma("pool", lambda e, dc=dc, c0=c0, c1=c1: e.dma_start(
                            out=wup[:, dc, c0:c1], in_=wup_d[dc * 128:(dc + 1) * 128, c0:c1]),
                            writes=[b_wupq[q][dc]])
                for c in crange:
                    S.dma("pool", lambda e, c=c: e.dma_start(out=wdn[:, c, :], in_=wdn_d[c * 128:(c + 1) * 128, :]),
                          writes=[b_wdn[c]])
            S.op("dve", lambda e: e.memset(halo[:].rearrange("p c t -> p (c t)"), 0.0), writes=b_halo)

            ck(10)
            def load_x1(j):
                xin, b_xin = xin_rot.get()
                S.dma("sp", lambda e, xin=xin, j=j: e.dma_start(out=xin[:], in_=x1_d[j * 128:(j + 1) * 128, :]),
                      reads=[b_x1[j]], writes=[b_xin])
                return xin, b_xin

            up_i = [0]
            LAG = 4
            D0 = 6
            prs = {}
            chains = {}

            def up_and_chain(mt, c, h2T, b_h2):
                ub = 4 + up_i[0]
                up_i[0] = (up_i[0] + 1) % 4
                pu = bank(ub)
                res = []
                for half, cch in ((0, c), (1, 22 + c)):
                    for dc in range(8):
                        S.op("pe", lambda e, pu=pu, half=half, cch=cch, dc=dc, h2T=h2T: e.matmul(
                            pu[:, half * T:(half + 1) * T], lhsT=wup[:, dc, cch * 128:(cch + 1) * 128],
                            rhs=h2T[:, dc, :], start=(dc == 0), stop=(dc == 7)),
                            reads=[b_wupq[(0 if cch < 11 else 2 if cch < 22 else 1 if cch < 33 else 3)][dc], b_h2], writes=[bk[ub]])
                for half, cch in ((0, c), (1, 22 + c)):
                    us, b_us = us_rot.get()
                    S.op("act", lambda e, pu=pu, half=half, us=us: e.activation(
                        out=us[:, 2:T + 2], in_=pu[:, half * T:(half + 1) * T], func=AF.Identity),
                        reads=[bk[ub]], writes=[b_us])
                    S.op("act", lambda e, us=us, cch=cch: e.activation(out=us[:, 0:2], in_=halo[:, cch, :], func=AF.Identity),
                         reads=[b_halo[cch]], writes=[b_us])
                    S.op("pool", lambda e, us=us, cch=cch: e.tensor_copy(out=halo[:, cch, :], in_=us[:, T:T + 2]),
                         reads=[b_us], writes=[b_halo[cch]])
                    a, b_a = f_rot.get()
                    w0 = colv[:, O_WFF + 0 * 44 + cch:O_WFF + 0 * 44 + cch + 1]
                    w1 = colv[:, O_WFF + 1 * 44 + cch:O_WFF + 1 * 44 + cch + 1]
                    w2 = colv[:, O_WFF + 2 * 44 + cch:O_WFF + 2 * 44 + cch + 1]
                    bc = colv[:, O_BFF + cch:O_BFF + cch + 1]
                    S.op("pool", lambda e, us=us, a=a, w0=w0, bc=bc: e.tensor_scalar(
                        out=a[:], in0=us[:, 0:T], scalar1=w0, scalar2=bc, op0=ALU.mult, op1=ALU.add),
                        reads=[b_us, b_colv], writes=[b_a])
                    S.op("dve", lambda e, us=us, a=a, w1=w1: e.scalar_tensor_tensor(
                        out=a[:], in0=us[:, 1:T + 1], scalar=w1, in1=a[:], op0=ALU.mult, op1=ALU.add),
                        reads=[b_us, b_colv, b_a], writes=[b_a])
                    S.op("dve", lambda e, us=us, a=a, w2=w2: e.scalar_tensor_tensor(
                        out=a[:], in0=us[:, 2:T + 2], scalar=w2, in1=a[:], op0=ALU.mult, op1=ALU.add),
                        reads=[b_us, b_colv, b_a], writes=[b_a])
                    res.append((a, b_a))
                chains[(mt, c)] = res

            def gelu_prod(mt, c):
                (va, b_va), (ga, b_ga) = chains.pop((mt, c))
                gl, b_gl = f_rot.get()
                S.op("act", lambda e, ga=ga, gl=gl: e.activation(out=gl[:], in_=ga[:], func=AF.Gelu),
                     reads=[b_ga], writes=[b_gl])
                pr, b_pr = pr_rot.get()
                S.op("dve", lambda e, gl=gl, va=va, pr=pr: e.tensor_tensor(out=pr[:], in0=gl[:], in1=va[:], op=ALU.mult),
                     reads=[b_gl, b_va], writes=[b_pr])
                prs[(mt, c)] = (pr, b_pr)

            def down(mt, c):
                pr, b_pr = prs.pop((mt, c))
                for s in range(2):
                    for half in range(2):
                        bi = 2 * s + half
                        S.op("pe", lambda e, bi=bi, s=s, half=half, pr=pr, c=c: e.matmul(
                            bank(bi), lhsT=pr[:, s * 128:(s + 1) * 128], rhs=wdn[:, c, half * 512:(half + 1) * 512],
                            start=(c == 0), stop=(c == 21)), reads=[b_pr, b_wdn[c]], writes=[bk[bi]])

            def postnorm(mt, cur):
                for s in range(2):
                    j = 2 * mt + s
                    x1t, b_x1t = cur[s]
                    pm = psum[:, 2 * s * 512:(2 * s + 2) * 512]
                    ss, b_ss = small_rot.get()
                    S.op("act", lambda e, pm=pm, ss=ss: e.activation(out=junk[:], in_=pm, func=AF.Square, accum_out=ss),
                         reads=[bk[2 * s], bk[2 * s + 1]], writes=[b_junk, b_ss])
                    r1, b_r1 = small_rot.get()
                    S.op("dve", lambda e, ss=ss, r1=r1: e.tensor_scalar(out=r1, in0=ss, scalar1=1.0 / D, scalar2=EPS,
                                                                        op0=ALU.mult, op1=ALU.add),
                         reads=[b_ss], writes=[b_r1])
                    r2, b_r2 = small_rot.get()
                    S.op("act", lambda e, r1=r1, r2=r2: e.activation(out=r2, in_=r1, func=AF.Sqrt), reads=[b_r1], writes=[b_r2])
                    S.op("dve", lambda e, r2=r2: e.reciprocal(out=r2, in_=r2), reads=[b_r2], writes=[b_r2])
                    S.op("dve", lambda e, pm=pm, r2=r2: e.scalar_tensor_tensor(
                        out=pm, in0=pm, scalar=r2, in1=GF[:], op0=ALU.mult, op1=ALU.mult),
                        reads=[bk[2 * s], bk[2 * s + 1], b_r2, b_GF], writes=[bk[2 * s], bk[2 * s + 1]])
                    S.op("dve", lambda e, pm=pm, x1t=x1t: e.tensor_tensor(out=x1t[:], in0=pm, in1=x1t[:], op=ALU.add),
                         reads=[bk[2 * s], bk[2 * s + 1], b_x1t], writes=[b_x1t])
                    S.dma("sp", lambda e, x1t=x1t, j=j: e.dma_start(out=out_d[j * 128:(j + 1) * 128, :], in_=x1t[:]),
                          reads=[b_x1t], writes=[b_out[j]])

            tiles_x = {0: [load_x1(0), load_x1(1)]}
            tiles_h = {0: h2_rot.get()}
            for s in range(2):
                prenorm(x1_d, s, tiles_x[0][s][0], tiles_x[0][s][1], xn_rot, tiles_h[0][0], tiles_h[0][1], s, A2, B2)
            pend = []
            G = 0
            seq = 0
            for mt in range(NT):
                h2T, b_h2 = tiles_h.pop(mt)
                for c in range(22):
                    up_and_chain(mt, c, h2T, b_h2)
                    if c >= 1:
                        gelu_prod(mt, c - 1)
                    elif mt >= 1:
                        gelu_prod(mt - 1, 21)
                    extra = max(0, D0 - c) if mt >= 1 else 0
                    pend.append((G + LAG + extra, seq, (lambda mt=mt, c=c: down(mt, c)))); seq += 1
                    if c == 21:
                        pend.append((G + LAG + extra, seq, (lambda mt=mt, cur=tiles_x[mt]: postnorm(mt, cur)))); seq += 1
                    ready = sorted([p_ for p_ in pend if p_[0] <= G], key=lambda p_: p_[1])
                    pend = [p_ for p_ in pend if p_[0] > G]
                    for p_ in ready:
                        p_[2]()
                    if c == 6 and mt + 1 < NT:
                        tiles_x[mt + 1] = [load_x1(2 * mt + 2), load_x1(2 * mt + 3)]
                    if c == 8 and mt + 1 < NT:
                        tiles_h[mt + 1] = h2_rot.get()
                        xas = [prenorm(x1_d, 2 * mt + 2 + s, tiles_x[mt + 1][s][0], tiles_x[mt + 1][s][1], xn_rot,
                                       tiles_h[mt + 1][0], tiles_h[mt + 1][1], s, A2, B2, split=True) for s in range(2)]
                    if c == 13 and mt + 1 < NT:
                        for s in range(2):
                            tb_ = 4 + up_i[0]
                            up_i[0] = (up_i[0] + 1) % 4
                            prenorm_b(xas[s][0], xas[s][1], tiles_h[mt + 1][0], tiles_h[mt + 1][1], s, A2, B2, tb=tb_)
                    G += 1
            gelu_prod(NT - 1, 21)
            for p_ in sorted(pend, key=lambda p_: p_[1]):
                p_[2]()
            S.finish("sp", b_out)
            S.barrier()
        S.emit()
    global LAST_SCHED
    LAST_SCHED = S
    return nc


def _prep_shared(inp):
    f = lambda a: np.ascontiguousarray(np.asarray(a, dtype=np.float32))
    col = lambda v: f(v).reshape(-1, 128).T
    colv = np.zeros((128, NCOLV), np.float32)
    colv[:, O_BADA:O_BADA + 48] = col(inp["b_ada"][0])
    colv[:, O_GPM:O_GPM + 8] = col(inp["g_pre_mix"][0])
    colv[:, O_GPF:O_GPF + 8] = col(inp["g_pre_ffn"][0])
    colv[:, O_BIN:O_BIN + 36] = col(inp["b_in"][0])
    wdw = f(inp["w_dw_conv"][0])
    for k in range(31):
        colv[:, O_WDW + k * 4:O_WDW + k * 4 + 4] = col(wdw[k])
    colv[:, O_BDW:O_BDW + 4] = col(inp["b_dw_conv"][0])
    colv[:, O_GLN:O_GLN + 4] = col(inp["g_conv_ln"][0])
    colv[:, O_BLN:O_BLN + 4] = col(inp["b_conv_ln"][0])
    colv[:, O_BCO:O_BCO + 8] = col(inp["b_conv_o"][0])
    wff = f(inp["w_dw_ffn"][0])
    for k in range(3):
        colv[:, O_WFF + k * 44:O_WFF + (k + 1) * 44] = col(wff[k])
    colv[:, O_BFF:O_BFF + 44] = col(inp["b_dw_ffn"][0])
    rowv = np.zeros((128, NROWV), np.float32)
    b_ada = f(inp["b_ada"][0])
    rowv[:, R_GPOSTM:R_GPOSTM + 1024] = f(inp["g_post_mix"][0])[None, :]
    rowv[:, R_GPOSTF:R_GPOSTF + 1024] = f(inp["g_post_ffn"][0])[None, :]
    rowv[:, R_BGTM:R_BGTM + 1024] = b_ada[2048:3072][None, :]
    rowv[:, R_BGTF:R_BGTF + 1024] = b_ada[5120:6144][None, :]
    rowv[:, R_BV:R_BV + 512] = f(inp["b_in"][0])[1024:1536][None, :]
    rb = f(inp["rel_bias"][0])
    key = np.arange(128)[:, None]
    q = np.arange(128)[None, :]
    biasT = np.zeros((128, 2, 8, 128), np.float32)
    for a, i in enumerate((3, 4)):
        rel = (4 - i) * 128 + q - key
        idx = np.clip(rel, -128, 128) + 128
        for h in range(8):
            biasT[:, a, h, :] = rb[h][idx]
    masked = (key >= 64) & (q < 64)
    biasT[:, 1, :, :][np.broadcast_to(masked[:, None, :], (128, 8, 128))] = NEG
    constb = np.broadcast_to(rb[:, 256][None, :], (128, 8)).astype(np.float32).copy()
    return dict(
        w_ada=f(inp["w_ada"][0]), w_in=f(inp["w_in"][0]), w_attn_o=f(inp["w_attn_o"][0]),
        w_conv_o=f(inp["w_conv_o"][0]), w_mix_o=f(inp["w_mix_o"][0]), w_up=f(inp["w_up"][0]),
        w_down=f(inp["w_down"][0]), colv=colv, rowv=rowv,
        biasT=np.ascontiguousarray(biasT.reshape(128, -1)), constb=constb)


_NC_CACHE = {}
LAST_SCHED = None


def kernel(**inputs):
    shared = _prep_shared(inputs)
    x = np.asarray(inputs["x"], dtype=np.float32)
    c = np.asarray(inputs["c"], dtype=np.float32)
    if "nc" not in _NC_CACHE:
        _NC_CACHE["nc"] = build_nc()
    nc = _NC_CACHE["nc"]
    in_maps = []
    for b in range(NCORES):
        m = dict(shared)
        m["x"] = np.ascontiguousarray(x[b])
        m["cT"] = np.ascontiguousarray(c[b].reshape(8, 128).T)
        in_maps.append(m)
    res = run_bass_kernel_spmd(nc, in_maps, core_ids=list(range(NCORES)))
    return np.stack([np.asarray(r["out"], dtype=np.float32) for r in res.results], axis=0)
```

```python
import numpy as np
from contextlib import ExitStack
import concourse.bass as bass
import concourse.mybir as mybir
from concourse.bass_utils import run_bass_kernel_spmd

F32 = mybir.dt.float32
BF16 = mybir.dt.bfloat16
AF = mybir.ActivationFunctionType
ALU = mybir.AluOpType

D = 1024
SEQ = 4096
NCORES = 8
DIN = 4608
DFF = 2816
EPS = 1e-6
T = 256
NT = SEQ // T
NEG = -30000.0

O_BADA, O_GPM, O_GPF, O_BIN, O_WDW, O_BDW, O_GLN, O_BLN, O_BCO, O_WFF, O_BFF = (
    0, 48, 56, 64, 100, 224, 228, 232, 236, 244, 376)
NCOLV = 420
R_GPOSTM, R_GPOSTF, R_BGTM, R_BGTF, R_BV = 0, 1024, 2048, 3072, 4096
NROWV = 4608


class Buf:
    __slots__ = ("name", "w", "r")

    def __init__(self, name=""):
        self.name = name
        self.w = None
        self.r = {}


class Sched:
    ENG = ("pe", "act", "dve", "pool", "sp")

    def __init__(self, nc, es, n_dma_sems=28):
        self.nc = nc
        self.prog = {e: [] for e in self.ENG}
        self.sem = {e: es.enter_context(nc.semaphore("s_" + e)) for e in self.ENG}
        self.cnt = {e: 0 for e in self.ENG}
        self.flag = {e: set() for e in self.ENG}
        self.dsem = [es.enter_context(nc.semaphore("d%d" % i)) for i in range(n_dma_sems)]
        self.dcnt = [0] * n_dma_sems
        self.dnx = {}
        self.known = {e: {} for e in self.ENG}

    def _need(self, e, key, val, waits):
        if key == e and e == "pe":
            return
        if self.known[e].get(key, 0) >= val:
            return
        if waits.get(key, 0) < val:
            waits[key] = val

    def _deps(self, e, reads, writes):
        waits = {}
        for b in reads:
            if b.w is not None:
                self._need(e, b.w[0], b.w[1], waits)
        for b in writes:
            if b.w is not None:
                self._need(e, b.w[0], b.w[1], waits)
            for k, v in b.r.items():
                self._need(e, k, v, waits)
        return waits

    def _emit_waits(self, e, waits):
        for key, val in waits.items():
            self.prog[e].append(("w", key, val))
            self.known[e][key] = val
            if isinstance(key, str):
                self.flag[key].add(val)

    def op(self, e, fn, reads=(), writes=()):
        self._emit_waits(e, self._deps(e, reads, writes))
        self.cnt[e] += 1
        v = self.cnt[e]
        self.prog[e].append(("o", fn, v))
        for b in reads:
            b.r[e] = v
        for b in writes:
            b.w = (e, v)
            b.r = {}
        return v

    def dma(self, e, fn, reads=(), writes=()):
        lo, hi = (0, 16) if e == "sp" else (16, len(self.dsem))
        i = self.dnx.get(e, lo)
        self.dnx[e] = lo + (i + 1 - lo) % (hi - lo)
        waits = self._deps(e, reads, writes)
        if self.dcnt[i] > 0:
            self._need(e, i, self.dcnt[i], waits)
        self._emit_waits(e, waits)
        self.dcnt[i] += 16
        v = self.dcnt[i]
        self.prog[e].append(("d", fn, i))
        for b in reads:
            b.r[i] = v
        for b in writes:
            b.w = (i, v)
            b.r = {}

    def barrier(self):
        for e in self.ENG:
            waits = {}
            for k in self.ENG:
                if k != e and self.cnt[k] > 0:
                    self._need(e, k, self.cnt[k], waits)
            for i, v in enumerate(self.dcnt):
                if v > 0:
                    self._need(e, i, v, waits)
            self._emit_waits(e, waits)

    def finish(self, e, bufs):
        waits = {}
        for b in bufs:
            if b.w is not None:
                self._need(e, b.w[0], b.w[1], waits)
        self._emit_waits(e, waits)

    def emit(self):
        nc = self.nc
        rank = {}
        for k in self.ENG:
            rank[k] = {v: i + 1 for i, v in enumerate(sorted(self.flag[k]))}

        def run(e, eng):
            for rec in self.prog[e]:
                if rec[0] == "w":
                    key, val = rec[1], rec[2]
                    if isinstance(key, str):
                        eng.wait_ge(self.sem[key], rank[key][val])
                    else:
                        eng.wait_ge(self.dsem[key], val)
                elif rec[0] == "o":
                    ins = rec[1](eng)
                    if rec[2] in rank[e]:
                        ins.then_inc(self.sem[e], 1)
                else:
                    rec[1](eng).then_inc(self.dsem[rec[2]], 16)

        with nc.Block() as block:
            @block.sync
            def _(eng):
                run("sp", eng)

            @block.tensor
            def _(eng):
                run("pe", eng)

            @block.scalar
            def _(eng):
                run("act", eng)

            @block.vector
            def _(eng):
                run("dve", eng)

            @block.gpsimd
            def _(eng):
                run("pool", eng)


class Rot:
    def __init__(self, tensors):
        self.items = [(t, Buf()) for t in tensors]
        self.i = 0

    def get(self):
        it = self.items[self.i]
        self.i = (self.i + 1) % len(self.items)
        return it


class StopBuild(Exception):
    pass


STOP = [None]
ATT_PIPE = True
import os
DENG = os.environ.get("DENG", "pool,pool,pool").split(",")


def build_nc(NT=NT):
    try:
        return _build_nc(NT)
    except StopBuild as ex:
        return ex.args[0]


def _build_nc(NT=NT):
    nc = bass.Bass("TRN2", target_bir_lowering=False)
    dt_in = lambda name, shape: nc.dram_tensor(name, shape, F32, kind="ExternalInput").ap()
    x_d = dt_in("x", [SEQ, D])
    cT_d = dt_in("cT", [128, 8])
    wada_d = dt_in("w_ada", [D, 6 * D])
    win_d = dt_in("w_in", [D, DIN])
    wao_d = dt_in("w_attn_o", [512, D])
    wco_d = dt_in("w_conv_o", [512, D])
    wmo_d = dt_in("w_mix_o", [D, D])
    wup_d = dt_in("w_up", [D, 2 * DFF])
    wdn_d = dt_in("w_down", [DFF, D])
    colv_d = dt_in("colv", [128, NCOLV])
    rowv_d = dt_in("rowv", [128, NROWV])
    biasT_d = dt_in("biasT", [128, 2 * 8 * 128])
    constb_d = dt_in("constb", [128, 8])
    out_d = nc.dram_tensor("out", [SEQ, D], F32, kind="ExternalOutput").ap()
    x1_d = nc.dram_tensor("x1s", [SEQ, D], F32, kind="Internal").ap()
    gf_d = nc.dram_tensor("gfs", [128, D], F32, kind="Internal").ap()
    bh_d = nc.dram_tensor("bhs", [128, 2048], BF16, kind="Internal").ap()
    bl_d = nc.dram_tensor("bls", [128, 2048], BF16, kind="Internal").ap()

    b_x1 = [Buf() for _ in range(SEQ // 128)]
    b_out = [Buf() for _ in range(SEQ // 128)]
    s_bufs = [Buf("S0"), Buf("S1")]

    with ExitStack() as es:
        S = Sched(nc, es)

        def ck(n):
            if STOP[0] == n:
                S.barrier()
                S.emit()
                global LAST_SCHED
                LAST_SCHED = S
                raise StopBuild(nc)
        sb = lambda st, name, shape, dt: st.enter_context(nc.sbuf_tensor("sb_" + name, shape, dt))

        colv = sb(es, "colv", [128, NCOLV], F32); b_colv = Buf()
        dv = sb(es, "dv", [128, 160], F32); b_dv = Buf()
        modc = sb(es, "modc", [128, 32], F32); b_modc = Buf()
        identb = sb(es, "identb", [128, 128], BF16); b_ident = Buf()
        onesf = sb(es, "onesf", [128, 128], F32); b_onesf = Buf()
        GP = sb(es, "GP", [128, D], F32); b_GP = Buf()
        b_GF = Buf(); b_gfd = Buf(); b_bhd = Buf(); b_bld = Buf()
        junk = sb(es, "junk", [128, D], BF16); b_junk = Buf()
        small = sb(es, "small", [128, 64], F32)
        small_rot = Rot([small[:, i:i + 1] for i in range(64)])
        psum = es.enter_context(nc.psum_tensor("ps_all", [128, 4096], F32))
        bk = [Buf("bank%d" % i) for i in range(8)]
        bank = lambda i: psum[:, i * 512:(i + 1) * 512]
        V_QB8, V_HGLB, V_HGAB, V_WDWH, V_GLNH, V_BLNH = 0, 4, 8, 24, 148, 152
        A1, B1, A2, B2 = 0, 8, 16, 24

        S.dma("sp", lambda e: e.dma_start(out=colv[:], in_=colv_d), writes=[b_colv])

        S.op("pool", lambda e: e.memset(onesf[:], 0.0), writes=[b_onesf])
        S.op("pool", lambda e: e.affine_select(out=onesf[:], in_=onesf[:], pattern=[[-1, 128]],
                                                compare_op=ALU.not_equal, fill=1.0, base=0,
                                                channel_multiplier=1),
             reads=[b_onesf], writes=[b_onesf])
        S.op("dve", lambda e: e.tensor_copy(out=identb[:], in_=onesf[:]), reads=[b_onesf], writes=[b_ident])
        S.op("pool", lambda e: e.memset(onesf[:], 1.0), reads=[b_onesf], writes=[b_onesf])

        def dcol(dst, n, src, mul):
            S.op("dve", lambda e: e.tensor_scalar(out=dv[:, dst:dst + n], in0=colv[:, src:src + n],
                                                  scalar1=mul, scalar2=None, op0=ALU.mult),
                 reads=[b_colv], writes=[b_dv])
        dcol(V_QB8, 4, O_BIN + 0, 0.125)
        dcol(V_HGLB, 4, O_BIN + 16, 0.5)
        dcol(V_HGAB, 16, O_BIN + 20, 0.5)
        dcol(V_WDWH, 124, O_WDW, 0.5)
        dcol(V_GLNH, 4, O_GLN, 0.5)
        dcol(V_BLNH, 4, O_BLN, 0.5)

        ck(1)
        with ExitStack() as s0:
            cT = sb(s0, "cT", [128, 8], F32); b_cT = Buf()
            cth = sb(s0, "cth", [128, 8], F32); b_cth = Buf()
            cact = sb(s0, "cact", [128, 8], F32); b_cact = Buf()
            cactb = sb(s0, "cactb", [128, 8], BF16); b_cactb = Buf()
            onesb = sb(s0, "onesb", [128, 128], BF16); b_onesb = Buf()
            crep = sb(s0, "crep", [128, 8, 128], BF16); b_crep = Buf()
            wa_rot = Rot([sb(s0, "wa%d" % i, [128, 8, 1024], BF16) for i in range(2)])
            rrow = Rot([sb(s0, "rr%d" % i, [128, 1024], F32) for i in range(2)])
            rtmp = sb(s0, "rtmp", [128, 1024], F32); b_rtmp = Buf()
            GFs = sb(s0, "GFs", [128, D], F32)

            S.dma("sp", lambda e: e.dma_start(out=cT[:], in_=cT_d), writes=[b_cT])
            S.op("act", lambda e: e.activation(out=cth[:], in_=cT[:], func=AF.Tanh, scale=0.5),
                 reads=[b_cT], writes=[b_cth])
            S.op("dve", lambda e: e.scalar_tensor_tensor(out=cact[:], in0=cth[:], scalar=1.0, in1=cT[:],
                                                         op0=ALU.add, op1=ALU.mult),
                 reads=[b_cth, b_cT], writes=[b_cact])
            S.op("dve", lambda e: e.tensor_scalar(out=cact[:], in0=cact[:], scalar1=0.5, scalar2=None,
                                                  op0=ALU.mult), reads=[b_cact], writes=[b_cact])
            S.op("dve", lambda e: e.tensor_copy(out=cactb[:], in_=cact[:]), reads=[b_cact], writes=[b_cactb])
            S.op("pool", lambda e: e.memset(onesb[:], 1.0), writes=[b_onesb])
            for dc in range(8):
                S.op("dve", lambda e, dc=dc: e.tensor_scalar(out=crep[:, dc, :], in0=onesb[:],
                                                             scalar1=cact[:, dc:dc + 1], scalar2=None,
                                                             op0=ALU.mult),
                     reads=[b_onesb, b_cact], writes=[b_crep])
            wada_v = wada_d.rearrange("(dc p) n -> p dc n", p=128)
            for g in range(6):
                wa, b_wa = wa_rot.get()
                S.dma("pool", lambda e, wa=wa, g=g: e.dma_start(out=wa[:], in_=wada_v[:, :, g * 1024:(g + 1) * 1024]),
                      writes=[b_wa])
                if g in (2, 5):
                    for half in range(2):
                        pb = bank(half)
                        for dc in range(8):
                            S.op("pe", lambda e, pb=pb, wa=wa, dc=dc, half=half: e.matmul(
                                pb, lhsT=crep[:, dc, :], rhs=wa[:, dc, half * 512:(half + 1) * 512],
                                start=(dc == 0), stop=(dc == 7)),
                                reads=[b_crep, b_wa], writes=[bk[half]])
                    r1, b_r1 = rrow.get()
                    r2, b_r2 = rrow.get()
                    o1 = R_BGTM if g == 2 else R_BGTF
                    o2 = R_GPOSTM if g == 2 else R_GPOSTF
                    S.dma("sp", lambda e, r1=r1, o1=o1: e.dma_start(out=r1[:], in_=rowv_d[:, o1:o1 + 1024]), writes=[b_r1])
                    S.dma("sp", lambda e, r2=r2, o2=o2: e.dma_start(out=r2[:], in_=rowv_d[:, o2:o2 + 1024]), writes=[b_r2])
                    G = GP if g == 2 else GFs
                    b_G = b_GP if g == 2 else b_GF
                    S.op("dve", lambda e, r1=r1: e.tensor_tensor(out=rtmp[:], in0=psum[:, 0:1024], in1=r1[:], op=ALU.add),
                         reads=[bk[0], bk[1], b_r1], writes=[b_rtmp])
                    S.op("dve", lambda e, r2=r2, G=G: e.tensor_tensor(out=G[:], in0=rtmp[:], in1=r2[:], op=ALU.mult),
                         reads=[b_rtmp, b_r2], writes=[b_G])
                else:
                    for oc in range(8):
                        col = 1024 + g * 8 + oc
                        for dc in range(8):
                            S.op("pe", lambda e, wa=wa, oc=oc, dc=dc, col=col: e.matmul(
                                psum[:, col:col + 1], lhsT=wa[:, dc, oc * 128:(oc + 1) * 128],
                                rhs=cactb[:, dc:dc + 1], start=(dc == 0), stop=(dc == 7)),
                                reads=[b_wa, b_cactb], writes=[bk[2]])
            mt_ = sb(s0, "modT", [128, 48], F32); b_mt = Buf()
            for c0 in (0, 24):
                S.op("dve", lambda e, c0=c0: e.tensor_tensor(out=mt_[:, c0:c0 + 16], in0=psum[:, 1024 + c0:1040 + c0],
                                                             in1=colv[:, O_BADA + c0:O_BADA + c0 + 16], op=ALU.add),
                     reads=[bk[2], b_colv], writes=[b_mt])
            S.op("dve", lambda e: e.scalar_tensor_tensor(out=modc[:, A1:A1 + 8], in0=mt_[:, 8:16], scalar=1.0,
                                                         in1=colv[:, O_GPM:O_GPM + 8], op0=ALU.add, op1=ALU.mult),
                 reads=[b_mt, b_colv], writes=[b_modc])
            S.op("dve", lambda e: e.tensor_copy(out=modc[:, B1:B1 + 8], in_=mt_[:, 0:8]), reads=[b_mt], writes=[b_modc])
            S.op("dve", lambda e: e.scalar_tensor_tensor(out=modc[:, A2:A2 + 8], in0=mt_[:, 32:40], scalar=1.0,
                                                         in1=colv[:, O_GPF:O_GPF + 8], op0=ALU.add, op1=ALU.mult),
                 reads=[b_mt, b_colv], writes=[b_modc])
            S.op("dve", lambda e: e.tensor_copy(out=modc[:, B2:B2 + 8], in_=mt_[:, 24:32]), reads=[b_mt], writes=[b_modc])
            S.dma("sp", lambda e: e.dma_start(out=gf_d, in_=GFs[:]), reads=[b_GF], writes=[b_gfd])
            bt32 = sb(s0, "bt32", [128, 2048], F32); b_bt32 = Buf()
            bhi = sb(s0, "bhi", [128, 2048], BF16); b_bhi = Buf()
            bhf = sb(s0, "bhf", [128, 2048], F32); b_bhf = Buf()
            blo = sb(s0, "blo", [128, 2048], BF16); b_blo = Buf()
            S.dma("sp", lambda e: e.dma_start(out=bt32[:], in_=biasT_d), writes=[b_bt32])
            S.op("dve", lambda e: e.tensor_copy(out=bhi[:], in_=bt32[:]), reads=[b_bt32], writes=[b_bhi])
            S.op("dve", lambda e: e.tensor_copy(out=bhf[:], in_=bhi[:]), reads=[b_bhi], writes=[b_bhf])
            S.op("dve", lambda e: e.tensor_tensor(out=blo[:], in0=bt32[:], in1=bhf[:], op=ALU.subtract),
                 reads=[b_bt32, b_bhf], writes=[b_blo])
            S.dma("sp", lambda e: e.dma_start(out=bh_d, in_=bhi[:]), reads=[b_bhi], writes=[b_bhd])
            S.dma("sp", lambda e: e.dma_start(out=bl_d, in_=blo[:]), reads=[b_blo], writes=[b_bld])
            S.barrier()

        ck(2)
        tp_bufs = [Buf("tp0"), Buf("tp1")]
        tpv = psum[:, 7 * 512:8 * 512].bitcast(BF16)
        tp_rot_i = [0]

        def prenorm(src_d, j, xin, b_xin, xn_rot, hT, b_hT, s, Acol, Bcol, split=False):
            ss, b_ss = small_rot.get()
            S.op("act", lambda e: e.activation(out=junk[:], in_=xin[:], func=AF.Square, accum_out=ss),
                 reads=[b_xin], writes=[b_junk, b_ss])
            r1, b_r1 = small_rot.get()
            S.op("dve", lambda e: e.tensor_scalar(out=r1, in0=ss, scalar1=1.0 / D, scalar2=EPS,
                                                  op0=ALU.mult, op1=ALU.add), reads=[b_ss], writes=[b_r1])
            r2, b_r2 = small_rot.get()
            S.op("act", lambda e: e.activation(out=r2, in_=r1, func=AF.Sqrt), reads=[b_r1], writes=[b_r2])
            S.op("dve", lambda e: e.reciprocal(out=r2, in_=r2), reads=[b_r2], writes=[b_r2])
            xn, b_xn = xn_rot.get()
            S.op("dve", lambda e: e.tensor_scalar(out=xn[:], in0=xin[:], scalar1=r2, scalar2=None, op0=ALU.mult),
                 reads=[b_xin, b_r2], writes=[b_xn])
            if split:
                return xn, b_xn
            prenorm_b(xn, b_xn, hT, b_hT, s, Acol, Bcol)

        def prenorm_b(xn, b_xn, hT, b_hT, s, Acol, Bcol, tb=7):
            tpv = psum[:, tb * 512:(tb + 1) * 512].bitcast(BF16)
            for dc in range(8):
                S.op("pe", lambda e, dc=dc, xn=xn: e.transpose(
                    tpv[:, dc * 128:(dc + 1) * 128], xn[:, dc * 128:(dc + 1) * 128], identb[:]),
                    reads=[b_xn, b_ident], writes=[bk[tb]])
            for dc in range(8):
                if dc % 2 == 0:
                    S.op("act", lambda e, dc=dc: e.activation(
                        out=hT[:, dc, s * 128:(s + 1) * 128], in_=tpv[:, dc * 128:(dc + 1) * 128],
                        func=AF.Identity, scale=modc[:, Acol + dc:Acol + dc + 1],
                        bias=modc[:, Bcol + dc:Bcol + dc + 1]),
                        reads=[bk[tb], b_modc], writes=[b_hT])
                else:
                    S.op("dve", lambda e, dc=dc: e.tensor_scalar(
                        out=hT[:, dc, s * 128:(s + 1) * 128], in0=tpv[:, dc * 128:(dc + 1) * 128],
                        scalar1=modc[:, Acol + dc:Acol + dc + 1], scalar2=modc[:, Bcol + dc:Bcol + dc + 1],
                        op0=ALU.mult, op1=ALU.add),
                        reads=[bk[tb], b_modc], writes=[b_hT])

        with ExitStack() as p1:
            win = sb(p1, "win", [128, 8, DIN], BF16)
            b_win = [Buf() for _ in range(4)]
            wao = sb(p1, "wao", [128, 4, D], BF16); b_wao = Buf()
            wco = sb(p1, "wco", [128, 4, D], BF16); b_wco = Buf()
            wmo = sb(p1, "wmo", [128, 8, D], BF16); b_wmo = Buf()
            Bh = sb(p1, "Bh", [128, 2, 8, 128], BF16); b_Bh = Buf()
            Bl = sb(p1, "Bl", [128, 2, 8, 128], BF16); b_Bl = Buf()
            Lm = sb(p1, "Lm", [1, 128], BF16); Rm = sb(p1, "Rm", [1, 128], BF16); b_LR = Buf()
            S.op("pool", lambda e: e.memset(Lm[:, 0:64], 1.0), writes=[b_LR])
            S.op("pool", lambda e: e.memset(Lm[:, 64:128], 0.0), writes=[b_LR])
            S.op("pool", lambda e: e.memset(Rm[:, 0:64], 0.0), writes=[b_LR])
            S.op("pool", lambda e: e.memset(Rm[:, 64:128], NEG), writes=[b_LR])
            constb = sb(p1, "constb", [128, 8], F32); b_cb = Buf()
            bvb = sb(p1, "bvb", [128, 512], F32); b_bvb = Buf()
            xin_rot = Rot([sb(p1, "xin%d" % i, [128, D], F32) for i in range(2)])
            xres_rot = Rot([sb(p1, "xres%d" % i, [128, D], F32) for i in range(1)])
            xn_rot = Rot([sb(p1, "xn%d" % i, [128, D], BF16) for i in range(1)])
            hT2 = [(sb(p1, "hT%d" % i, [128, 8, T], BF16), Buf()) for i in range(2)]
            QT2 = [(sb(p1, "QT%d" % i, [128, 4, T], BF16), Buf()) for i in range(2)]
            KT = sb(p1, "KT", [128, 4, 1024], BF16)
            b_KT = [Buf() for _ in range(4)]
            Vr = sb(p1, "Vr", [128, 8, 8, 65], BF16)
            b_Vr = [Buf() for _ in range(8)]
            u2 = [(sb(p1, "u%d" % i, [128, 4, 30 + T], BF16), [Buf() for _ in range(4)]) for i in range(2)]
            cacc = sb(p1, "cacc", [128, 4, T], F32)
            b_cacc = [Buf() for _ in range(4)]
            stA = sb(p1, "stA", [128, T], F32); b_stA = Buf()
            stB = sb(p1, "stB", [128, T], F32); b_stB = Buf()
            uT2 = [(sb(p1, "uT%d" % i, [128, 4, T], BF16), Buf()) for i in range(2)]
            f_rot = Rot([sb(p1, "f%d" % i, [128, T], F32) for i in range(9)])
            PT_rot = Rot([sb(p1, "PT%d" % i, [128, 5, 128], BF16) for i in range(2)])
            rc_t = sb(p1, "rc", [128, 16], F32)
            rc_rot = Rot([rc_t[:, i * 4:(i + 1) * 4] for i in range(4)])
            attn_rot = Rot([sb(p1, "attn%d" % i, [128, 512], BF16) for i in range(1)])
            attnT2 = [(sb(p1, "attnT%d" % i, [128, 4, T], BF16), Buf()) for i in range(2)]
            tg_rot = Rot([sb(p1, "tg%d" % i, [128, 2, T], F32) for i in range(1)])
            yT2 = [(sb(p1, "yT%d" % i, [128, 8, T], BF16), Buf()) for i in range(2)]

            win_v = win_d.rearrange("(dc p) n -> p dc n", p=128)
            groups = [(0, 1536), (1536, 2560), (2560, 3584), (3584, 4608)]
            for gi, (c0, c1) in enumerate(groups):
                S.dma("pool", lambda e, c0=c0, c1=c1: e.dma_start(out=win[:, :, c0:c1], in_=win_v[:, :, c0:c1]),
                      writes=[b_win[gi]])
            S.dma("sp", lambda e: e.dma_start(out=Bh[:].rearrange("p a h q -> p (a h q)"), in_=bh_d), reads=[b_bhd], writes=[b_Bh])
            S.dma("sp", lambda e: e.dma_start(out=Bl[:].rearrange("p a h q -> p (a h q)"), in_=bl_d), reads=[b_bld], writes=[b_Bl])
            S.dma("sp", lambda e: e.dma_start(out=constb[:], in_=constb_d), writes=[b_cb])
            S.dma("sp", lambda e: e.dma_start(out=bvb[:], in_=rowv_d[:, R_BV:R_BV + 512]), writes=[b_bvb])
            S.dma("pool", lambda e: e.dma_start(out=wao[:], in_=wao_d.rearrange("(c p) n -> p c n", p=128)), writes=[b_wao])
            S.dma("pool", lambda e: e.dma_start(out=wco[:], in_=wco_d.rearrange("(c p) n -> p c n", p=128)), writes=[b_wco])
            S.dma("pool", lambda e: e.dma_start(out=wmo[:], in_=wmo_d.rearrange("(c p) n -> p c n", p=128)), writes=[b_wmo])
            S.op("pool", lambda e: e.memset(Vr[:].rearrange("p a h d -> p (a h d)"), 1.0), writes=b_Vr)
            S.op("pool", lambda e: e.memset(u2[0][0][:, :, 0:30], 0.0), writes=u2[0][1])

            ck(3)
            mm_i = [0]

            MMB = [0, 1, 7]

            def mmidx():
                i = MMB[mm_i[0]]
                mm_i[0] = (mm_i[0] + 1) % len(MMB)
                return i

            def mmbank():
                i = mmidx()
                return bank(i), bk[i]

            def load_x(j):
                xin, b_xin = xin_rot.get()
                S.dma("sp", lambda e, xin=xin, j=j: e.dma_start(out=xin[:], in_=x_d[j * 128:(j + 1) * 128, :]),
                      writes=[b_xin])
                return xin, b_xin

            def tile_gen(mt):
                par = mt % 2
                hT, b_hT = hT2[par]
                QT, b_QT = QT2[par]
                u, b_u = u2[par]
                u_o, b_uo = u2[1 - par]
                uT, b_uT = uT2[par]
                attnT, b_attnT = attnT2[par]
                yT, b_yT = yT2[par]
                cur = [load_x(2 * mt), load_x(2 * mt + 1)]
                ring = mt % 4
                for s in range(2):
                    xa = prenorm(x_d, 2 * mt + s, cur[s][0], cur[s][1], xn_rot, hT, b_hT, s, A1, B1, split=True)
                    yield 0
                    prenorm_b(xa[0], xa[1], hT, b_hT, s, A1, B1, tb=mmidx())
                    yield 0
                def qk_group(oc):
                    pb, bb = mmbank()
                    for dc in range(8):
                        S.op("pe", lambda e, pb=pb, oc=oc, dc=dc: e.matmul(
                            pb[:, 0:T], lhsT=win[:, dc, oc * 128:(oc + 1) * 128], rhs=hT[:, dc, :],
                            start=(dc == 0), stop=(dc == 7)), reads=[b_win[0], b_hT], writes=[bb])
                    if oc < 4:
                        S.op("act", lambda e, pb=pb, oc=oc: e.activation(
                            out=QT[:, oc, :], in_=pb[:, 0:T], func=AF.Identity, scale=0.125,
                            bias=dv[:, V_QB8 + oc:V_QB8 + oc + 1]), reads=[bb, b_dv], writes=[b_QT])
                    else:
                        S.op("act", lambda e, pb=pb, oc=oc, ring=ring: e.activation(
                            out=KT[:, oc - 4, ring * T:(ring + 1) * T], in_=pb[:, 0:T], func=AF.Identity,
                            bias=colv[:, O_BIN + oc:O_BIN + oc + 1]),
                            reads=[bb, b_colv], writes=[b_KT[ring]])

                def v_group(s):
                    j = 2 * mt + s
                    pb, bb = mmbank()
                    for dc in range(8):
                        S.op("pe", lambda e, pb=pb, dc=dc, s=s: e.matmul(
                            pb, lhsT=hT[:, dc, s * 128:(s + 1) * 128], rhs=win[:, dc, 1024:1536],
                            start=(dc == 0), stop=(dc == 7)), reads=[b_win[0], b_hT], writes=[bb])
                    S.op("dve", lambda e, pb=pb, j=j: e.tensor_tensor(
                        out=Vr[:, j % 8, :, 0:64], in0=pb.rearrange("p (h d) -> p h d", h=8),
                        in1=bvb[:].rearrange("p (h d) -> p h d", h=8), op=ALU.add),
                        reads=[bb, b_bvb], writes=[b_Vr[j % 8]])

                deferred = [(lambda oc=oc: qk_group(oc)) for oc in range(8)] + [(lambda s=s: v_group(s)) for s in range(2)]
                for cc in range(4):
                    pb, bb = mmbank()
                    for half, base in ((0, 1536), (1, 2048)):
                        for dc in range(8):
                            S.op("pe", lambda e, pb=pb, cc=cc, dc=dc, half=half, base=base: e.matmul(
                                pb[:, half * T:(half + 1) * T],
                                lhsT=win[:, dc, base + cc * 128:base + (cc + 1) * 128], rhs=hT[:, dc, :],
                                start=(dc == 0), stop=(dc == 7)), reads=[b_win[1], b_hT], writes=[bb])
                    tgl, b_tgl = f_rot.get()
                    S.op("act", lambda e, pb=pb, cc=cc, tgl=tgl: e.activation(
                        out=tgl[:], in_=pb[:, T:2 * T], func=AF.Tanh, scale=0.5,
                        bias=dv[:, V_HGLB + cc:V_HGLB + cc + 1]), reads=[bb, b_dv], writes=[b_tgl])
                    ab, b_ab = f_rot.get()
                    S.op("act", lambda e, pb=pb, cc=cc, ab=ab: e.activation(
                        out=ab[:], in_=pb[:, 0:T], func=AF.Identity,
                        bias=colv[:, O_BIN + 12 + cc:O_BIN + 13 + cc]), reads=[bb, b_colv], writes=[b_ab])
                    S.op("dve", lambda e, cc=cc, tgl=tgl, ab=ab: e.scalar_tensor_tensor(
                        out=u[:, cc, 30:30 + T], in0=tgl[:], scalar=1.0, in1=ab[:], op0=ALU.add, op1=ALU.mult),
                        reads=[b_tgl, b_ab], writes=[b_u[cc]])
                    yield 0
                yield 1
                for k in range(31):
                    for cc in range(4):
                        eng = "dve"
                        wcol = dv[:, V_WDWH + k * 4 + cc:V_WDWH + k * 4 + cc + 1]
                        if k == 0:
                            S.op(eng, lambda e, cc=cc, wcol=wcol: e.tensor_scalar(
                                out=cacc[:, cc, :], in0=u[:, cc, 0:T], scalar1=wcol,
                                scalar2=colv[:, O_BDW + cc:O_BDW + cc + 1], op0=ALU.mult, op1=ALU.add),
                                reads=[b_u[cc], b_dv, b_colv], writes=[b_cacc[cc]])
                        else:
                            S.op(eng, lambda e, cc=cc, k=k, wcol=wcol: e.scalar_tensor_tensor(
                                out=cacc[:, cc, :], in0=u[:, cc, k:k + T], scalar=wcol, in1=cacc[:, cc, :],
                                op0=ALU.mult, op1=ALU.add),
                                reads=[b_u[cc], b_dv, b_cacc[cc]], writes=[b_cacc[cc]])
                    if k % 3 == 1 and deferred:
                        deferred.pop(0)()
                    yield 1
                while deferred:
                    deferred.pop(0)()
                for cc in range(4):
                    eng = "dve" if cc < 2 else "pool"
                    S.op(eng, lambda e, cc=cc: e.tensor_copy(out=u_o[:, cc, 0:30], in_=u[:, cc, T:T + 30]),
                         reads=[b_u[cc]], writes=[b_uo[cc]])
                pb, bb = mmbank()
                for cc in range(4):
                    S.op("pe", lambda e, pb=pb, cc=cc: e.matmul(pb[:, 0:T], lhsT=onesf[:], rhs=cacc[:, cc, :],
                                                                 start=(cc == 0), stop=(cc == 3)),
                         reads=[b_onesf, b_cacc[cc]], writes=[bb])
                for cc in range(4):
                    sq, b_sq = f_rot.get()
                    S.op("act", lambda e, cc=cc, sq=sq: e.activation(out=sq[:], in_=cacc[:, cc, :], func=AF.Square),
                         reads=[b_cacc[cc]], writes=[b_sq])
                    S.op("pe", lambda e, pb=pb, cc=cc, sq=sq: e.matmul(pb[:, T:2 * T], lhsT=onesf[:], rhs=sq[:],
                                                                        start=(cc == 0), stop=(cc == 3)),
                         reads=[b_onesf, b_sq], writes=[bb])
                mean, b_mean = f_rot.get()
                S.op("dve", lambda e, pb=pb, mean=mean: e.tensor_scalar(out=mean[:], in0=pb[:, 0:T], scalar1=1.0 / 512,
                                                                        scalar2=None, op0=ALU.mult),
                     reads=[bb], writes=[b_mean])
                var, b_var = f_rot.get()
                S.op("dve", lambda e, mean=mean, var=var: e.tensor_tensor(out=var[:], in0=mean[:], in1=mean[:], op=ALU.mult),
                     reads=[b_mean], writes=[b_var])
                S.op("dve", lambda e, pb=pb, var=var: e.scalar_tensor_tensor(
                    out=var[:], in0=pb[:, T:2 * T], scalar=1.0 / 512, in1=var[:], op0=ALU.mult, op1=ALU.subtract),
                    reads=[bb, b_var], writes=[b_var])
                S.op("dve", lambda e, var=var: e.tensor_scalar(out=var[:], in0=var[:], scalar1=EPS, scalar2=None,
                                                                op0=ALU.add), reads=[b_var], writes=[b_var])
                S.op("act", lambda e, var=var: e.activation(out=stA[:], in_=var[:], func=AF.Sqrt),
                     reads=[b_var], writes=[b_stA])
                S.op("dve", lambda e: e.reciprocal(out=stA[:], in_=stA[:]), reads=[b_stA], writes=[b_stA])
                S.op("dve", lambda e, mean=mean: e.scalar_tensor_tensor(
                    out=stB[:], in0=mean[:], scalar=-1.0, in1=stA[:], op0=ALU.mult, op1=ALU.mult),
                    reads=[b_mean, b_stA], writes=[b_stB])
                yield 1
                for cc in range(4):
                    eng = "dve" if cc < 2 else "pool"
                    S.op(eng, lambda e, cc=cc: e.tensor_tensor(out=cacc[:, cc, :], in0=cacc[:, cc, :], in1=stA[:], op=ALU.mult),
                         reads=[b_cacc[cc], b_stA], writes=[b_cacc[cc]])
                    S.op(eng, lambda e, cc=cc: e.tensor_tensor(out=cacc[:, cc, :], in0=cacc[:, cc, :], in1=stB[:], op=ALU.add),
                         reads=[b_cacc[cc], b_stB], writes=[b_cacc[cc]])
                    yh, b_yh = f_rot.get()
                    S.op("act", lambda e, cc=cc, yh=yh: e.activation(
                        out=yh[:], in_=cacc[:, cc, :], func=AF.Identity, scale=dv[:, V_GLNH + cc:V_GLNH + cc + 1],
                        bias=dv[:, V_BLNH + cc:V_BLNH + cc + 1]), reads=[b_cacc[cc], b_dv], writes=[b_yh])
                    th, b_th = f_rot.get()
                    S.op("act", lambda e, yh=yh, th=th: e.activation(out=th[:], in_=yh[:], func=AF.Tanh),
                         reads=[b_yh], writes=[b_th])
                    S.op("dve", lambda e, cc=cc, yh=yh, th=th: e.scalar_tensor_tensor(
                        out=uT[:, cc, :], in0=th[:], scalar=1.0, in1=yh[:], op0=ALU.add, op1=ALU.mult),
                        reads=[b_th, b_yh], writes=[b_uT])
                    yield 1
                yield 2
                for s in range(2):
                    j = 2 * mt + s
                    i0 = max(0, 4 - j)
                    at, b_at = attn_rot.get()
                    pv = psum[:, 6 * 512:6 * 512 + 260].rearrange("p (h d) -> p h d", d=65)

                    def emit_S(h, s=s, j=j, i0=i0):
                        hp, po = h // 2, (h % 2) * 64
                        so = 1024 + (h % 2) * 1024
                        for i in range(i0, 5):
                            kb = j - 4 + i
                            rs, ro = (kb // 2) % 4, (kb % 2) * 128
                            S.op("pe", lambda e, so=so, i=i, po=po, hp=hp, rs=rs, ro=ro, s=s: e.matmul(
                                psum[:, so + i * 128:so + (i + 1) * 128],
                                lhsT=KT[po:po + 64, hp, rs * T + ro:rs * T + ro + 128],
                                rhs=QT[po:po + 64, hp, s * 128:(s + 1) * 128], start=True, stop=(0 < i < 3 or (i == 0 and i0 > 0))),
                                reads=[b_KT[rs], b_QT], writes=[s_bufs[h % 2]])
                            if i == 0 and i0 == 0:
                                S.op("pe", lambda e, so=so: e.matmul(
                                    psum[:, so:so + 128], lhsT=Lm[:], rhs=Rm[:], start=False, stop=True),
                                    reads=[b_LR], writes=[s_bufs[h % 2]])
                            if i >= 3:
                                S.op("pe", lambda e, so=so, i=i, h=h: e.matmul(
                                    psum[:, so + i * 128:so + (i + 1) * 128], lhsT=identb[:], rhs=Bh[:, i - 3, h, :],
                                    start=False, stop=False), reads=[b_ident, b_Bh], writes=[s_bufs[h % 2]])
                                S.op("pe", lambda e, so=so, i=i, h=h: e.matmul(
                                    psum[:, so + i * 128:so + (i + 1) * 128], lhsT=identb[:], rhs=Bl[:, i - 3, h, :],
                                    start=False, stop=True), reads=[b_ident, b_Bl], writes=[s_bufs[h % 2]])

                    def emit_soft(h, i0=i0):
                        so = 1024 + (h % 2) * 1024
                        bS = s_bufs[h % 2]
                        PT, b_PT = PT_rot.get()
                        if i0 < 3:
                            S.op("act", lambda e, so=so, i0=i0, PT=PT, h=h: e.activation(
                                out=PT[:, i0:3, :],
                                in_=psum[:, so + i0 * 128:so + 3 * 128].rearrange("p (a q) -> p a q", q=128),
                                func=AF.Exp, bias=constb[:, h:h + 1]), reads=[bS, b_cb], writes=[b_PT])
                        t0 = max(i0, 3)
                        S.op("act", lambda e, so=so, t0=t0, PT=PT: e.activation(
                            out=PT[:, t0:5, :],
                            in_=psum[:, so + t0 * 128:so + 5 * 128].rearrange("p (a q) -> p a q", q=128),
                            func=AF.Exp), reads=[bS, b_PT], writes=[b_PT])
                        return PT, b_PT

                    def emit_PV(h, PT, b_PT, j=j, i0=i0, pv=pv):
                        for i in range(i0, 5):
                            kb = j - 4 + i
                            S.op("pe", lambda e, pv=pv, PT=PT, i=i, kb=kb, h=h, i0=i0: e.matmul(
                                pv[:, h % 4, :], lhsT=PT[:, i, :], rhs=Vr[:, kb % 8, h, :],
                                start=(i == i0), stop=(i == 4)),
                                reads=[b_PT, b_Vr[kb % 8]], writes=[bk[6]])

                    if ATT_PIPE:
                        emit_S(0)
                    for h in range(8):
                        if not ATT_PIPE:
                            emit_S(h)
                        PT, b_PT = emit_soft(h)
                        if ATT_PIPE and h + 1 < 8:
                            emit_S(h + 1)
                        emit_PV(h, PT, b_PT)
                        if h % 4 == 3:
                            half = h // 4
                            rc, b_rc = rc_rot.get()
                            S.op("dve", lambda e, pv=pv, rc=rc: e.reciprocal(out=rc, in_=pv[:, :, 64]),
                                 reads=[bk[6]], writes=[b_rc])
                            S.op("dve", lambda e, pv=pv, rc=rc, at=at, half=half: e.tensor_tensor(
                                out=at[:, half * 256:(half + 1) * 256].rearrange("p (h d) -> p h d", d=64),
                                in0=pv[:, :, 0:64], in1=rc.unsqueeze(2).to_broadcast([128, 4, 64]), op=ALU.mult),
                                reads=[bk[6], b_rc, b_at], writes=[b_at])
                        yield 2
                    tb_ = mmidx()
                    tpa = psum[:, tb_ * 512:(tb_ + 1) * 512].bitcast(BF16)
                    for c4 in range(4):
                        S.op("pe", lambda e, c4=c4, at=at, tpa=tpa: e.transpose(
                            tpa[:, c4 * 128:(c4 + 1) * 128], at[:, c4 * 128:(c4 + 1) * 128], identb[:]),
                            reads=[b_at, b_ident], writes=[bk[tb_]])
                    S.op("act", lambda e, s=s, tpa=tpa: e.activation(
                        out=attnT[:, :, s * 128:(s + 1) * 128], in_=tpa[:, 0:512].rearrange("p (c q) -> p c q", q=128),
                        func=AF.Identity), reads=[bk[tb_]], writes=[b_attnT])
                    yield 2
                yield 3
                for fc in range(8):
                    pg, bg = mmbank()
                    for half, base, gi in ((0, 2560, 2), (1, 3584, 3)):
                        for dc in range(8):
                            S.op("pe", lambda e, pg=pg, fc=fc, dc=dc, half=half, base=base: e.matmul(
                                pg[:, half * T:(half + 1) * T],
                                lhsT=win[:, dc, base + fc * 128:base + (fc + 1) * 128], rhs=hT[:, dc, :],
                                start=(dc == 0), stop=(dc == 7)), reads=[b_win[gi], b_hT], writes=[bg])
                    pa, ba = mmbank()
                    for c4 in range(4):
                        S.op("pe", lambda e, pa=pa, fc=fc, c4=c4: e.matmul(
                            pa[:, 0:T], lhsT=wao[:, c4, fc * 128:(fc + 1) * 128], rhs=attnT[:, c4, :],
                            start=(c4 == 0), stop=(c4 == 3)), reads=[b_wao, b_attnT], writes=[ba])
                    for c4 in range(4):
                        S.op("pe", lambda e, pa=pa, fc=fc, c4=c4: e.matmul(
                            pa[:, T:2 * T], lhsT=wco[:, c4, fc * 128:(fc + 1) * 128], rhs=uT[:, c4, :],
                            start=(c4 == 0), stop=(c4 == 3)), reads=[b_wco, b_uT], writes=[ba])
                    tg, b_tg = tg_rot.get()
                    for half in range(2):
                        S.op("act", lambda e, pg=pg, tg=tg, half=half, fc=fc: e.activation(
                            out=tg[:, half, :], in_=pg[:, half * T:(half + 1) * T], func=AF.Tanh, scale=0.5,
                            bias=dv[:, V_HGAB + half * 8 + fc:V_HGAB + half * 8 + fc + 1]),
                            reads=[bg, b_dv], writes=[b_tg])
                    asb, b_asb = f_rot.get()
                    S.op("act", lambda e, pa=pa, asb=asb: e.activation(out=asb[:], in_=pa[:, 0:T], func=AF.Identity),
                         reads=[ba], writes=[b_asb])
                    cbb, b_cbb = f_rot.get()
                    S.op("act", lambda e, pa=pa, cbb=cbb, fc=fc: e.activation(
                        out=cbb[:], in_=pa[:, T:2 * T], func=AF.Identity,
                        bias=colv[:, O_BCO + fc:O_BCO + fc + 1]), reads=[ba, b_colv], writes=[b_cbb])
                    t1, b_t1 = f_rot.get()
                    S.op("dve", lambda e, tg=tg, asb=asb, t1=t1: e.scalar_tensor_tensor(
                        out=t1[:], in0=tg[:, 0, :], scalar=1.0, in1=asb[:], op0=ALU.add, op1=ALU.mult),
                        reads=[b_tg, b_asb], writes=[b_t1])
                    t2, b_t2 = f_rot.get()
                    S.op("dve", lambda e, tg=tg, cbb=cbb, t2=t2: e.scalar_tensor_tensor(
                        out=t2[:], in0=tg[:, 1, :], scalar=1.0, in1=cbb[:], op0=ALU.add, op1=ALU.mult),
                        reads=[b_tg, b_cbb], writes=[b_t2])
                    S.op("dve", lambda e, t1=t1, t2=t2, fc=fc: e.tensor_tensor(
                        out=yT[:, fc, :], in0=t1[:], in1=t2[:], op=ALU.add),
                        reads=[b_t1, b_t2], writes=[b_yT])
                    yield 3
                yield 4
                for s in range(2):
                    j = 2 * mt + s
                    xres, b_xres = xres_rot.get()
                    S.dma("sp", lambda e, xres=xres, j=j: e.dma_start(out=xres[:], in_=x_d[j * 128:(j + 1) * 128, :]),
                          writes=[b_xres])
                    pm = psum[:, 2 * 512:4 * 512]
                    for half in range(2):
                        for dc in range(8):
                            S.op("pe", lambda e, pm=pm, half=half, dc=dc, s=s: e.matmul(
                                pm[:, half * 512:(half + 1) * 512], lhsT=yT[:, dc, s * 128:(s + 1) * 128],
                                rhs=wmo[:, dc, half * 512:(half + 1) * 512], start=(dc == 0), stop=(dc == 7)),
                                reads=[b_yT, b_wmo], writes=[s_bufs[0]])
                    ss, b_ss = small_rot.get()
                    S.op("act", lambda e, pm=pm, ss=ss: e.activation(out=junk[:], in_=pm, func=AF.Square, accum_out=ss),
                         reads=[s_bufs[0]], writes=[b_junk, b_ss])
                    r1, b_r1 = small_rot.get()
                    S.op("dve", lambda e, ss=ss, r1=r1: e.tensor_scalar(out=r1, in0=ss, scalar1=1.0 / D, scalar2=4 * EPS,
                                                                        op0=ALU.mult, op1=ALU.add),
                         reads=[b_ss], writes=[b_r1])
                    r2, b_r2 = small_rot.get()
                    S.op("act", lambda e, r1=r1, r2=r2: e.activation(out=r2, in_=r1, func=AF.Sqrt), reads=[b_r1], writes=[b_r2])
                    S.op("dve", lambda e, r2=r2: e.reciprocal(out=r2, in_=r2), reads=[b_r2], writes=[b_r2])
                    S.op("dve", lambda e, pm=pm, r2=r2: e.scalar_tensor_tensor(
                        out=pm, in0=pm, scalar=r2, in1=GP[:], op0=ALU.mult, op1=ALU.mult),
                        reads=[s_bufs[0], b_r2, b_GP], writes=[s_bufs[0]])
                    S.op("dve", lambda e, pm=pm, xres=xres: e.tensor_tensor(out=xres[:], in0=pm, in1=xres[:], op=ALU.add),
                         reads=[s_bufs[0], b_xres], writes=[b_xres])
                    S.dma("sp", lambda e, xres=xres, j=j: e.dma_start(out=x1_d[j * 128:(j + 1) * 128, :], in_=xres[:]),
                          reads=[b_xres], writes=[b_x1[j]])
                    yield 4

            active = []
            st = {}
            next_mt = 0
            while active or next_mt < NT:
                can_start = next_mt < NT and (
                    not active
                    or (len(active) == 1 and st[active[0]] >= 1)
                    or (len(active) == 2 and st[active[0]] >= 4 and st[active[1]] >= 1))
                if can_start:
                    g = tile_gen(next_mt)
                    next_mt += 1
                    active.append(g)
                    st[g] = 0
                for g in list(active):
                    k = active.index(g)
                    if k == 0 or st[g] < st[active[k - 1]]:
                        try:
                            st[g] = next(g)
                        except StopIteration:
                            active.remove(g)
                            del st[g]
            ck(9)
            S.barrier()

        with ExitStack() as p2:
            wup = sb(p2, "wup", [128, 8, 2 * DFF], BF16)
            b_wupq = [[Buf() for _ in range(8)] for _ in range(4)]
            wdn = sb(p2, "wdn", [128, 22, D], BF16)
            b_wdn = [Buf() for _ in range(22)]
            xin_rot = Rot([sb(p2, "x2in%d" % i, [128, D], F32) for i in range(4)])
            xn_rot = Rot([sb(p2, "x2n%d" % i, [128, D], BF16) for i in range(2)])
            h2_rot = Rot([sb(p2, "h2T%d" % i, [128, 8, T], BF16) for i in range(2)])
            us_rot = Rot([sb(p2, "us%d" % i, [128, T + 2], F32) for i in range(8)])
            halo = sb(p2, "halo", [128, 44, 2], F32); b_halo = [Buf() for _ in range(44)]
            f_rot = Rot([sb(p2, "g%d" % i, [128, T], F32) for i in range(15)])
            pr_rot = Rot([sb(p2, "pr%d" % i, [128, T], BF16) for i in range(13)])
            GF = sb(p2, "GF", [128, D], F32); b_GF = Buf()
            S.dma("sp", lambda e: e.dma_start(out=GF[:], in_=gf_d), reads=[b_gfd], writes=[b_GF])

            QCOL = [(0, 1408), (2816, 4224), (1408, 2816), (4224, 5632)]
            for qs, crange in (((0, 1), range(0, 11)), ((2, 3), range(11, 22))):
                for q in qs:
                    c0, c1 = QCOL[q]
                    for dc in range(8):
                        S.dma("pool", lambda e, dc=dc, c0=c0, c1=c1: e.dma_start(
                            out=wup[:, dc, c0:c1], in_=wup_d[dc * 128:(dc + 1) * 128, c0:c1]),
                            writes=[b_wupq[q][dc]])
                for c in crange:
                    S.dma("pool", lambda e, c=c: e.dma_start(out=wdn[:, c, :], in_=wdn_d[c * 128:(c + 1) * 128, :]),
                          writes=[b_wdn[c]])
            S.op("dve", lambda e: e.memset(halo[:].rearrange("p c t -> p (c t)"), 0.0), writes=b_halo)

            ck(10)
            def load_x1(j):
                xin, b_xin = xin_rot.get()
                S.dma("sp", lambda e, xin=xin, j=j: e.dma_start(out=xin[:], in_=x1_d[j * 128:(j + 1) * 128, :]),
                      reads=[b_x1[j]], writes=[b_xin])
                return xin, b_xin

            up_i = [0]
            LAG = 4
            D0 = 4
            prs = {}
            chains = {}

            def up_and_chain(mt, c, h2T, b_h2):
                ub = 4 + up_i[0]
                up_i[0] = (up_i[0] + 1) % 4
                pu = bank(ub)
                res = []
                for half, cch in ((0, c), (1, 22 + c)):
                    for dc in range(8):
                        S.op("pe", lambda e, pu=pu, half=half, cch=cch, dc=dc, h2T=h2T: e.matmul(
                            pu[:, half * T:(half + 1) * T], lhsT=wup[:, dc, cch * 128:(cch + 1) * 128],
                            rhs=h2T[:, dc, :], start=(dc == 0), stop=(dc == 7)),
                            reads=[b_wupq[(0 if cch < 11 else 2 if cch < 22 else 1 if cch < 33 else 3)][dc], b_h2], writes=[bk[ub]])
                for half, cch in ((0, c), (1, 22 + c)):
                    us, b_us = us_rot.get()
                    S.op("act", lambda e, pu=pu, half=half, us=us: e.activation(
                        out=us[:, 2:T + 2], in_=pu[:, half * T:(half + 1) * T], func=AF.Identity),
                        reads=[bk[ub]], writes=[b_us])
                    S.op("act", lambda e, us=us, cch=cch: e.activation(out=us[:, 0:2], in_=halo[:, cch, :], func=AF.Identity),
                         reads=[b_halo[cch]], writes=[b_us])
                    S.op("pool", lambda e, us=us, cch=cch: e.tensor_copy(out=halo[:, cch, :], in_=us[:, T:T + 2]),
                         reads=[b_us], writes=[b_halo[cch]])
                    a, b_a = f_rot.get()
                    w0 = colv[:, O_WFF + 0 * 44 + cch:O_WFF + 0 * 44 + cch + 1]
                    w1 = colv[:, O_WFF + 1 * 44 + cch:O_WFF + 1 * 44 + cch + 1]
                    w2 = colv[:, O_WFF + 2 * 44 + cch:O_WFF + 2 * 44 + cch + 1]
                    bc = colv[:, O_BFF + cch:O_BFF + cch + 1]
                    S.op("pool", lambda e, us=us, a=a, w0=w0, bc=bc: e.tensor_scalar(
                        out=a[:], in0=us[:, 0:T], scalar1=w0, scalar2=bc, op0=ALU.mult, op1=ALU.add),
                        reads=[b_us, b_colv], writes=[b_a])
                    S.op("dve", lambda e, us=us, a=a, w1=w1: e.scalar_tensor_tensor(
                        out=a[:], in0=us[:, 1:T + 1], scalar=w1, in1=a[:], op0=ALU.mult, op1=ALU.add),
                        reads=[b_us, b_colv, b_a], writes=[b_a])
                    S.op("dve", lambda e, us=us, a=a, w2=w2: e.scalar_tensor_tensor(
                        out=a[:], in0=us[:, 2:T + 2], scalar=w2, in1=a[:], op0=ALU.mult, op1=ALU.add),
                        reads=[b_us, b_colv, b_a], writes=[b_a])
                    res.append((a, b_a))
                chains[(mt, c)] = res

            def gelu_prod(mt, c):
                (va, b_va), (ga, b_ga) = chains.pop((mt, c))
                gl, b_gl = f_rot.get()
                S.op("act", lambda e, ga=ga, gl=gl: e.activation(out=gl[:], in_=ga[:], func=AF.Gelu),
                     reads=[b_ga], writes=[b_gl])
                pr, b_pr = pr_rot.get()
                S.op("dve", lambda e, gl=gl, va=va, pr=pr: e.tensor_tensor(out=pr[:], in0=gl[:], in1=va[:], op=ALU.mult),
                     reads=[b_gl, b_va], writes=[b_pr])
                prs[(mt, c)] = (pr, b_pr)

            def down(mt, c):
                pr, b_pr = prs.pop((mt, c))
                for s in range(2):
                    for half in range(2):
                        bi = 2 * s + half
                        S.op("pe", lambda e, bi=bi, s=s, half=half, pr=pr, c=c: e.matmul(
                            bank(bi), lhsT=pr[:, s * 128:(s + 1) * 128], rhs=wdn[:, c, half * 512:(half + 1) * 512],
                            start=(c == 0), stop=(c == 21)), reads=[b_pr, b_wdn[c]], writes=[bk[bi]])

            def postnorm(mt, cur):
                for s in range(2):
                    j = 2 * mt + s
                    x1t, b_x1t = cur[s]
                    pm = psum[:, 2 * s * 512:(2 * s + 2) * 512]
                    ss, b_ss = small_rot.get()
                    S.op("act", lambda e, pm=pm, ss=ss: e.activation(out=junk[:], in_=pm, func=AF.Square, accum_out=ss),
                         reads=[bk[2 * s], bk[2 * s + 1]], writes=[b_junk, b_ss])
                    r1, b_r1 = small_rot.get()
                    S.op("dve", lambda e, ss=ss, r1=r1: e.tensor_scalar(out=r1, in0=ss, scalar1=1.0 / D, scalar2=EPS,
                                                                        op0=ALU.mult, op1=ALU.add),
                         reads=[b_ss], writes=[b_r1])
                    r2, b_r2 = small_rot.get()
                    S.op("act", lambda e, r1=r1, r2=r2: e.activation(out=r2, in_=r1, func=AF.Sqrt), reads=[b_r1], writes=[b_r2])
                    S.op("dve", lambda e, r2=r2: e.reciprocal(out=r2, in_=r2), reads=[b_r2], writes=[b_r2])
                    S.op("dve", lambda e, pm=pm, r2=r2: e.scalar_tensor_tensor(
                        out=pm, in0=pm, scalar=r2, in1=GF[:], op0=ALU.mult, op1=ALU.mult),
                        reads=[bk[2 * s], bk[2 * s + 1], b_r2, b_GF], writes=[bk[2 * s], bk[2 * s + 1]])
                    S.op("dve", lambda e, pm=pm, x1t=x1t: e.tensor_tensor(out=x1t[:], in0=pm, in1=x1t[:], op=ALU.add),
                         reads=[bk[2 * s], bk[2 * s + 1], b_x1t], writes=[b_x1t])
                    S.dma("sp", lambda e, x1t=x1t, j=j: e.dma_start(out=out_d[j * 128:(j + 1) * 128, :], in_=x1t[:]),
                          reads=[b_x1t], writes=[b_out[j]])

            tiles_x = {0: [load_x1(0), load_x1(1)]}
            tiles_h = {0: h2_rot.get()}
            for s in range(2):
                prenorm(x1_d, s, tiles_x[0][s][0], tiles_x[0][s][1], xn_rot, tiles_h[0][0], tiles_h[0][1], s, A2, B2)
            pend = []
            G = 0
            seq = 0
            for mt in range(NT):
                h2T, b_h2 = tiles_h.pop(mt)
                for c in range(22):
                    up_and_chain(mt, c, h2T, b_h2)
                    if c >= 1:
                        gelu_prod(mt, c - 1)
                    elif mt >= 1:
                        gelu_prod(mt - 1, 21)
                    extra = max(0, D0 - c) if mt >= 1 else 0
                    pend.append((G + LAG + extra, seq, (lambda mt=mt, c=c: down(mt, c)))); seq += 1
                    if c == 21:
                        pend.append((G + LAG + extra, seq, (lambda mt=mt, cur=tiles_x[mt]: postnorm(mt, cur)))); seq += 1
                    ready = sorted([p_ for p_ in pend if p_[0] <= G], key=lambda p_: p_[1])
                    pend = [p_ for p_ in pend if p_[0] > G]
                    for p_ in ready:
                        p_[2]()
                    if c == 6 and mt + 1 < NT:
                        tiles_x[mt + 1] = [load_x1(2 * mt + 2), load_x1(2 * mt + 3)]
                    if c == 8 and mt + 1 < NT:
                        tiles_h[mt + 1] = h2_rot.get()
                        xas = [prenorm(x1_d, 2 * mt + 2 + s, tiles_x[mt + 1][s][0], tiles_x[mt + 1][s][1], xn_rot,
                                       tiles_h[mt + 1][0], tiles_h[mt + 1][1], s, A2, B2, split=True) for s in range(2)]
                    if c == 13 and mt + 1 < NT:
                        for s in range(2):
                            tb_ = 4 + up_i[0]
                            up_i[0] = (up_i[0] + 1) % 4
                            prenorm_b(xas[s][0], xas[s][1], tiles_h[mt + 1][0], tiles_h[mt + 1][1], s, A2, B2, tb=tb_)
                    G += 1
            gelu_prod(NT - 1, 21)
            for p_ in sorted(pend, key=lambda p_: p_[1]):
                p_[2]()
            S.finish("sp", b_out)
            S.barrier()
        S.emit()
    global LAST_SCHED
    LAST_SCHED = S
    return nc


def _prep_shared(inp):
    f = lambda a: np.ascontiguousarray(np.asarray(a, dtype=np.float32))
    col = lambda v: f(v).reshape(-1, 128).T
    colv = np.zeros((128, NCOLV), np.float32)
    colv[:, O_BADA:O_BADA + 48] = col(inp["b_ada"][0])
    colv[:, O_GPM:O_GPM + 8] = col(inp["g_pre_mix"][0])
    colv[:, O_GPF:O_GPF + 8] = col(inp["g_pre_ffn"][0])
    colv[:, O_BIN:O_BIN + 36] = col(inp["b_in"][0])
    wdw = f(inp["w_dw_conv"][0])
    for k in range(31):
        colv[:, O_WDW + k * 4:O_WDW + k * 4 + 4] = col(wdw[k])
    colv[:, O_BDW:O_BDW + 4] = col(inp["b_dw_conv"][0])
    colv[:, O_GLN:O_GLN + 4] = col(inp["g_conv_ln"][0])
    colv[:, O_BLN:O_BLN + 4] = col(inp["b_conv_ln"][0])
    colv[:, O_BCO:O_BCO + 8] = col(inp["b_conv_o"][0])
    wff = f(inp["w_dw_ffn"][0])
    for k in range(3):
        colv[:, O_WFF + k * 44:O_WFF + (k + 1) * 44] = col(wff[k])
    colv[:, O_BFF:O_BFF + 44] = col(inp["b_dw_ffn"][0])
    rowv = np.zeros((128, NROWV), np.float32)
    b_ada = f(inp["b_ada"][0])
    rowv[:, R_GPOSTM:R_GPOSTM + 1024] = f(inp["g_post_mix"][0])[None, :]
    rowv[:, R_GPOSTF:R_GPOSTF + 1024] = f(inp["g_post_ffn"][0])[None, :]
    rowv[:, R_BGTM:R_BGTM + 1024] = b_ada[2048:3072][None, :]
    rowv[:, R_BGTF:R_BGTF + 1024] = b_ada[5120:6144][None, :]
    rowv[:, R_BV:R_BV + 512] = f(inp["b_in"][0])[1024:1536][None, :]
    rb = f(inp["rel_bias"][0])
    key = np.arange(128)[:, None]
    q = np.arange(128)[None, :]
    biasT = np.zeros((128, 2, 8, 128), np.float32)
    for a, i in enumerate((3, 4)):
        rel = (4 - i) * 128 + q - key
        idx = np.clip(rel, -128, 128) + 128
        for h in range(8):
            biasT[:, a, h, :] = rb[h][idx]
    masked = (key >= 64) & (q < 64)
    biasT[:, 1, :, :][np.broadcast_to(masked[:, None, :], (128, 8, 128))] = NEG
    constb = np.broadcast_to(rb[:, 256][None, :], (128, 8)).astype(np.float32).copy()
    return dict(
        w_ada=f(inp["w_ada"][0]), w_in=f(inp["w_in"][0]), w_attn_o=f(inp["w_attn_o"][0]),
        w_conv_o=f(inp["w_conv_o"][0]), w_mix_o=f(inp["w_mix_o"][0]), w_up=f(inp["w_up"][0]),
        w_down=f(inp["w_down"][0]), colv=colv, rowv=rowv,
        biasT=np.ascontiguousarray(biasT.reshape(128, -1)), constb=constb)


_NC_CACHE = {}
LAST_SCHED = None


def kernel(**inputs):
    shared = _prep_shared(inputs)
    x = np.asarray(inputs["x"], dtype=np.float32)
    c = np.asarray(inputs["c"], dtype=np.float32)
    if "nc" not in _NC_CACHE:
        _NC_CACHE["nc"] = build_nc()
    nc = _NC_CACHE["nc"]
    in_maps = []
    for b in range(NCORES):
        m = dict(shared)
        m["x"] = np.ascontiguousarray(x[b])
        m["cT"] = np.ascontiguousarray(c[b].reshape(8, 128).T)
        in_maps.append(m)
    res = run_bass_kernel_spmd(nc, in_maps, core_ids=list(range(NCORES)))
    return np.stack([np.asarray(r["out"], dtype=np.float32) for r in res.results], axis=0)
```
